# Optimizing a Trainium2 kernel written in Bass

```python
import math
import jax, jax.numpy as jnp
from jax import lax
import numpy as np

D_MODEL = 2048
BATCH = 1
SEQ = 8192
DEPTH = 1

D_MIX = D_MODEL
D_RG = D_MIX // 2
D_S5 = D_MIX - D_RG
RG_HEADS = 16
RG_HEAD_DIM = D_RG // RG_HEADS
RG_CONV_W = 4
RG_C = 8.0
S5_GROUP = 16
S5_GROUPS = D_S5 // S5_GROUP
S5_STATE = 64
DT_MIN = 1e-3
DT_MAX = 1e-1
D_FF = 3 * D_MODEL
FFN_CONV_W = 3
EPS = 1e-6
D_IN = 2 * D_RG + D_S5

kernel_name = "hymba_style_rglru_s5_convffn_block"


def rmsnorm(x, g):
    xf = x.astype(jnp.float32)
    y = xf * lax.rsqrt(jnp.mean(xf * xf, axis=-1, keepdims=True) + EPS)
    return (y * g.astype(jnp.float32)).astype(x.dtype)


def causal_dwconv(x, w, b):
    k = w.shape[0]
    c = x.shape[-1]
    y = lax.conv_general_dilated(
        x, w[:, None, :].astype(x.dtype), window_strides=(1,),
        padding=[(k - 1, 0)], dimension_numbers=("NWC", "WIO", "NWC"),
        feature_group_count=c)
    return y + b.astype(x.dtype)


def linear_scan(a, b):
    def combine(left, right):
        a_l, b_l = left
        a_r, b_r = right
        return a_l * a_r, a_r * b_l + b_r
    _, h = lax.associative_scan(combine, (a, b), axis=1)
    return h


def rg_lru_group(u, gate, conv_w, conv_b, w_a, b_a, w_i, b_i, lam):
    bsz, s, _ = u.shape
    u = causal_dwconv(u, conv_w, conv_b)
    uf = u.astype(jnp.float32)
    uh = uf.reshape(bsz, s, RG_HEADS, RG_HEAD_DIM)
    r = jax.nn.sigmoid(jnp.einsum("bshi,hij->bshj", uh, w_a.astype(jnp.float32)).reshape(bsz, s, D_RG)
                       + b_a.astype(jnp.float32))
    i = jax.nn.sigmoid(jnp.einsum("bshi,hij->bshj", uh, w_i.astype(jnp.float32)).reshape(bsz, s, D_RG)
                       + b_i.astype(jnp.float32))
    log_a = -RG_C * r * jax.nn.softplus(-lam.astype(jnp.float32))
    a = jnp.exp(log_a)
    mult = jnp.sqrt(-jnp.expm1(2.0 * log_a))
    h = linear_scan(a, mult * (i * uf))
    return (jax.nn.gelu(gate.astype(jnp.float32)) * h).astype(u.dtype)


def s5_group(u, log_dt, a_re, a_im, b_re, b_im, c_re, c_im, d, w_glu, b_glu):
    bsz, s, _ = u.shape
    uf = u.astype(jnp.float32)
    ug = uf.reshape(bsz, s, S5_GROUPS, S5_GROUP)
    dt = jnp.exp(log_dt.astype(jnp.float32))[:, None]
    A = lax.complex(a_re.astype(jnp.float32), a_im.astype(jnp.float32))
    A_bar = jnp.exp(A * dt)
    Bm = lax.complex(b_re.astype(jnp.float32), b_im.astype(jnp.float32))
    Cm = lax.complex(c_re.astype(jnp.float32), c_im.astype(jnp.float32))
    B_bar = ((A_bar - 1.0) / A)[..., None] * Bm
    Bu = jnp.einsum("bsgh,gph->bsgp", ug.astype(jnp.complex64), B_bar)
    a = jnp.broadcast_to(A_bar, Bu.shape)
    h = linear_scan(a, Bu)
    y = jnp.real(jnp.einsum("ghp,bsgp->bsgh", Cm, h)).reshape(bsz, s, D_S5)
    y = y + d.astype(jnp.float32) * uf
    z = jax.nn.gelu(y)
    out = z * jax.nn.sigmoid(z @ w_glu.astype(jnp.float32) + b_glu.astype(jnp.float32))
    return out.astype(u.dtype)


def conv_gated_mlp(x, w_up, conv_w, conv_b, w_down):
    h = causal_dwconv(x @ w_up, conv_w, conv_b)
    g, v = jnp.split(h, 2, axis=-1)
    return (jax.nn.gelu(g) * v) @ w_down


def setup_inputs(seed: int = 0) -> dict:
    key = jax.random.key(seed)
    ks = jax.random.split(key, 32)
    f = jnp.float32
    L = DEPTH

    def nrm(k, shape, scale):
        return jax.random.normal(k, shape, f) * scale

    a8 = jax.random.uniform(ks[9], (L, D_RG), f, 0.9, 0.999)
    sig = a8 ** (1.0 / RG_C)
    n = jnp.arange(S5_STATE, dtype=f)
    return {
        "x": nrm(ks[0], (BATCH, SEQ, D_MODEL), 1.0),
        "norm1_g": 1.0 + nrm(ks[1], (L, D_MODEL), 0.02),
        "w_in": nrm(ks[2], (L, D_MODEL, D_IN), D_MODEL ** -0.5),
        "rg_conv_w": nrm(ks[3], (L, RG_CONV_W, D_RG), RG_CONV_W ** -0.5),
        "rg_conv_b": nrm(ks[4], (L, D_RG), 0.01),
        "rg_wa": nrm(ks[5], (L, RG_HEADS, RG_HEAD_DIM, RG_HEAD_DIM), RG_HEAD_DIM ** -0.5),
        "rg_ba": nrm(ks[6], (L, D_RG), 0.01),
        "rg_wi": nrm(ks[7], (L, RG_HEADS, RG_HEAD_DIM, RG_HEAD_DIM), RG_HEAD_DIM ** -0.5),
        "rg_bi": nrm(ks[8], (L, D_RG), 0.01),
        "rg_lambda": jnp.log(sig) - jnp.log1p(-sig),
        "s5_log_dt": jax.random.uniform(ks[10], (L, S5_GROUPS), f, math.log(DT_MIN), math.log(DT_MAX)),
        "s5_a_re": -0.5 + nrm(ks[11], (L, S5_GROUPS, S5_STATE), 0.01),
        "s5_a_im": math.pi * n + nrm(ks[12], (L, S5_GROUPS, S5_STATE), 0.01),
        "s5_b_re": nrm(ks[13], (L, S5_GROUPS, S5_STATE, S5_GROUP), (2 * S5_GROUP) ** -0.5),
        "s5_b_im": nrm(ks[14], (L, S5_GROUPS, S5_STATE, S5_GROUP), (2 * S5_GROUP) ** -0.5),
        "s5_c_re": nrm(ks[15], (L, S5_GROUPS, S5_GROUP, S5_STATE), S5_STATE ** -0.5),
        "s5_c_im": nrm(ks[16], (L, S5_GROUPS, S5_GROUP, S5_STATE), S5_STATE ** -0.5),
        "s5_d": nrm(ks[17], (L, D_S5), 1.0),
        "s5_w_glu": nrm(ks[18], (L, D_S5, D_S5), D_S5 ** -0.5),
        "s5_b_glu": nrm(ks[19], (L, D_S5), 0.01),
        "out_norm_rg_g": 1.0 + nrm(ks[20], (L, D_RG), 0.02),
        "out_norm_s5_g": 1.0 + nrm(ks[21], (L, D_S5), 0.02),
        "w_out": nrm(ks[22], (L, D_MIX, D_MODEL), D_MIX ** -0.5),
        "norm2_g": 1.0 + nrm(ks[23], (L, D_MODEL), 0.02),
        "ffn_w_up": nrm(ks[24], (L, D_MODEL, 2 * D_FF), D_MODEL ** -0.5),
        "ffn_conv_w": nrm(ks[25], (L, FFN_CONV_W, 2 * D_FF), FFN_CONV_W ** -0.5),
        "ffn_conv_b": nrm(ks[26], (L, 2 * D_FF), 0.01),
        "ffn_w_down": nrm(ks[27], (L, D_FF, D_MODEL), D_FF ** -0.5),
        "final_norm_g": 1.0 + nrm(ks[28], (D_MODEL,), 0.02),
    }


def reference(x, norm1_g, w_in, rg_conv_w, rg_conv_b, rg_wa, rg_ba, rg_wi, rg_bi, rg_lambda,
              s5_log_dt, s5_a_re, s5_a_im, s5_b_re, s5_b_im, s5_c_re, s5_c_im, s5_d,
              s5_w_glu, s5_b_glu, out_norm_rg_g, out_norm_s5_g, w_out, norm2_g,
              ffn_w_up, ffn_conv_w, ffn_conv_b, ffn_w_down, final_norm_g):
    for l in range(DEPTH):
        xn = rmsnorm(x, norm1_g[l])
        proj = xn @ w_in[l]
        rg_u = proj[..., :D_RG]
        rg_gate = proj[..., D_RG:2 * D_RG]
        s5_u = proj[..., 2 * D_RG:]
        y_rg = rg_lru_group(rg_u, rg_gate, rg_conv_w[l], rg_conv_b[l], rg_wa[l], rg_ba[l],
                            rg_wi[l], rg_bi[l], rg_lambda[l])
        y_s5 = s5_group(s5_u, s5_log_dt[l], s5_a_re[l], s5_a_im[l], s5_b_re[l], s5_b_im[l],
                        s5_c_re[l], s5_c_im[l], s5_d[l], s5_w_glu[l], s5_b_glu[l])
        mixed = jnp.concatenate([rmsnorm(y_rg, out_norm_rg_g[l]),
                                 rmsnorm(y_s5, out_norm_s5_g[l])], axis=-1)
        x = x + mixed @ w_out[l]
        x = x + conv_gated_mlp(rmsnorm(x, norm2_g[l]), ffn_w_up[l], ffn_conv_w[l],
                               ffn_conv_b[l], ffn_w_down[l])
    return rmsnorm(x, final_norm_g)
```

```python
import numpy as np
import concourse.bass as bass
import concourse.mybir as mybir
from concourse.bass_utils import run_bass_kernel_spmd

F32 = mybir.dt.float32
BF16 = mybir.dt.bfloat16
I32 = mybir.dt.int32
AF = mybir.ActivationFunctionType
ALU = mybir.AluOpType

NCORE = 8
D = 2048
T = 1024
DRG = 1024
DS5 = 1024
DFF = 6144
EPS = 1e-6
TWO_PI = 6.283185307179586
SB_BASE = 16640
SB_END = 229376

VEC_ITEMS = [("n1g", 16), ("rgcw", 32), ("rgcb", 8), ("ba", 8), ("bi", 8), ("lam", 8),
             ("s5d", 8), ("bglu", 8), ("grg", 8), ("gs5", 8), ("n2g", 16),
             ("fcw", 288), ("fcb", 96), ("fng", 16)]
VOFF = {}
_o = 0
for _n, _c in VEC_ITEMS:
    VOFF[_n] = _o
    _o += _c
NV = _o
C_ID, C_BD, C_IOTA, C_M, C_OHS, C_OHP, C_M96, NCST = 0, 128, 256, 384, 388, 396, 404, 405


class Prog:
    def __init__(self, nc, debug=()):
        self.nc = nc
        self.debug = set(debug)
        self.dumps = []
        self.ops = {e: [] for e in ("pe", "act", "dve", "pool", "sp")}
        self.cnt = {e: 0 for e in ("pe", "act", "dve", "pool")}
        self.sems = {}
        self.W = {}
        self.R = {}
        self.waited = {e: {} for e in self.ops}
        self.tinfo = {}
        self.dma_n = {"sp": 0, "pool": 0}
        self.RING = 6
        self.sb_top = SB_BASE
        self.cc_n = 0
        self.nps = 0

    def sb(self, name, shape, dtype, off=None):
        esz = 2 if dtype == BF16 else 4
        nbytes = int(np.prod(shape[1:])) * esz
        if off is None:
            off = (self.sb_top + 31) // 32 * 32
            self.sb_top = off + nbytes
            assert self.sb_top <= SB_END, f"SBUF overflow at {name}: {self.sb_top}"
        else:
            assert off + nbytes <= SB_END, f"SBUF overflow at {name}"
        t = self.nc.alloc_sbuf_tensor_at(f"{name}_{len(self.tinfo)}", list(shape), dtype, offset=off)
        self.tinfo[t.name] = ("sb", off, nbytes, esz)
        return t

    def mark(self):
        return self.sb_top

    def release(self, m):
        self.sb_top = m

    def psum(self, name):
        t = self.nc.alloc_psum_tensor(name, [128, 512], F32)
        self.tinfo[t.name] = ("ps", self.nps * 2048, 2048, 4)
        self.nps += 1
        return t

    def dram(self, name, shape, dtype=F32, kind=None):
        if kind is None:
            t = self.nc.dram_tensor(name, list(shape), dtype)
        else:
            t = self.nc.dram_tensor(name, list(shape), dtype, kind=kind)
        self.tinfo[t.name] = ("dr", name, 0, 4)
        return t

    def blocks(self, ap):
        info = self.tinfo[ap.tensor.name]
        if info[0] == "dr":
            return [("dr", info[1])]
        space, base, nbytes, _ = info
        esz = 2 if ap.dtype == BF16 else 4
        tfree = nbytes // esz
        aps = ap.ap
        off = ap.offset % tfree
        hi = off
        for (st, cn) in aps[1:]:
            if st > 0:
                hi += st * (cn - 1)
        lo_b = base + off * esz
        hi_b = base + (hi + 1) * esz
        return [(space, b) for b in range(lo_b // 256, (hi_b - 1) // 256 + 1)]

    def sem(self, name):
        if name not in self.sems:
            self.sems[name] = self.nc.alloc_semaphore(name)
        return self.sems[name]

    def _record(self, eng, fn, reads, writes, kind):
        waits = {}

        def need(tok):
            if tok is not None:
                s, v = tok
                if waits.get(s, 0) < v:
                    waits[s] = v
        rb = [b for ap in reads for b in self.blocks(ap)]
        wb = [b for ap in writes for b in self.blocks(ap)]
        for b in rb:
            need(self.W.get(b))
        for b in wb:
            need(self.W.get(b))
            for s, v in self.R.get(b, {}).items():
                need((s, v))
        if kind == "compute":
            self.cnt[eng] += 1
            tok = ("c_" + eng, self.cnt[eng])
            inc = 1
        elif kind == "dma":
            n = self.dma_n[eng]
            self.dma_n[eng] += 1
            sname = f"d_{eng}_{n % self.RING}"
            tok = (sname, 16 * (n // self.RING + 1))
            if n >= self.RING:
                need((sname, 16 * (n // self.RING)))
            inc = 16
        else:
            self.cc_n += 1
            tok = (f"cc{self.cc_n}", 1)
            inc = 0
        if eng == "pe":
            waits.pop("c_pe", None)
        wl = []
        for s, v in waits.items():
            if self.waited[eng].get(s, 0) < v:
                self.waited[eng][s] = v
                wl.append((s, v))
        for b in rb:
            d = self.R.setdefault(b, {})
            if d.get(tok[0], 0) < tok[1]:
                d[tok[0]] = tok[1]
        for b in wb:
            self.W[b] = tok
            self.R[b] = {}
        for s, _ in wl:
            self.sem(s)
        self.sem(tok[0])
        self.ops[eng].append((wl, fn, tok[0], inc))

    def op(self, eng, fn, reads, writes):
        self._record(eng, fn, reads, writes, "compute")

    def dma(self, eng, out, in_, **kw):
        self._record(eng, lambda e: e.dma_start(out=out, in_=in_, **kw), [in_], [out], "dma")

    def collective(self, ib, ob):
        def fn(e):
            return e.collective_compute("AllGather", ALU.bypass, replica_groups=[list(range(NCORE))],
                                        ins=[ib.ap().opt()], outs=[ob.ap().opt()])
        self._record("pool", fn, [ib.ap()], [ob.ap()], "cc")

    def dump(self, name, ap, dtype=F32):
        if name not in self.debug:
            return
        shape = list(ap.shape)
        t = self.dram("dbg_" + name, shape, dtype, kind="ExternalOutput")
        self.dumps.append("dbg_" + name)
        self.dma("sp", t.ap(), ap)

    def act(self, out, in_, func, bias=None, scale=1.0, accum_out=None, extra_reads=()):
        reads = [in_] + list(extra_reads)
        if not isinstance(bias, (int, float, type(None))):
            reads.append(bias)
        if not isinstance(scale, (int, float)):
            reads.append(scale)
        writes = [out] + ([accum_out] if accum_out is not None else [])
        kw = {}
        if bias is not None:
            kw["bias"] = bias
        if accum_out is not None:
            kw["accum_out"] = accum_out
        self.op("act", lambda e: e.activation(out=out, in_=in_, func=func, scale=scale, **kw), reads, writes)

    def tt(self, eng, out, in0, in1, op):
        self.op(eng, lambda e: e.tensor_tensor(out=out, in0=in0, in1=in1, op=op), [in0, in1], [out])

    def ts(self, eng, out, in0, s1, s2=None, op0=ALU.mult, op1=None):
        reads = [in0] + [s for s in (s1, s2) if not isinstance(s, (int, float, type(None)))]
        if op1 is None:
            self.op(eng, lambda e: e.tensor_scalar(out=out, in0=in0, scalar1=s1, scalar2=None, op0=op0), reads, [out])
        else:
            self.op(eng, lambda e: e.tensor_scalar(out=out, in0=in0, scalar1=s1, scalar2=s2, op0=op0, op1=op1), reads, [out])

    def stt(self, eng, out, in0, scalar, in1, op0, op1):
        reads = [in0, in1] + ([] if isinstance(scalar, (int, float)) else [scalar])
        eng = "dve"
        self.op(eng, lambda e: e.scalar_tensor_tensor(out=out, in0=in0, scalar=scalar, in1=in1, op0=op0, op1=op1), reads, [out])

    def copy(self, eng, out, in_):
        if eng == "act":
            self.act(out, in_, AF.Copy)
        else:
            self.op(eng, lambda e: e.tensor_copy(out=out, in_=in_), [in_], [out])

    def scan(self, out, d0, d1, init):
        reads = [d0, d1] + ([] if isinstance(init, (int, float)) else [init])
        self.op("dve", lambda e: e.tensor_tensor_scan(out=out, data0=d0, data1=d1, initial=init, op0=ALU.mult, op1=ALU.add), reads, [out])

    def mm(self, out, lhsT, rhs, start, stop):
        self.op("pe", lambda e: e.matmul(out, lhsT=lhsT, rhs=rhs, start=start, stop=stop, skip_group_check=True), [lhsT, rhs], [out])

    def transpose(self, out, in_, ident):
        self.op("pe", lambda e: e.transpose(out, in_, ident), [in_, ident], [out])

    def memset(self, eng, ap, val):
        self.op(eng, lambda e: e.memset(ap, val), [], [ap])

    def emit(self):
        nc = self.nc
        sems = self.sems
        with nc.Block() as block:
            def run(name):
                def body(e):
                    for (wl, fn, tsem, inc) in self.ops[name]:
                        for s, v in wl:
                            e.wait_ge(sems[s], v)
                        if inc == 0:
                            fn(e).then_inc(sems[tsem])
                        else:
                            fn(e).then_inc(sems[tsem], inc)
                    if name in ("sp", "pool"):
                        n = self.dma_n[name]
                        for r in range(min(n, self.RING)):
                            last = ((n - 1 - r) // self.RING) if n - 1 >= r else -1
                            cntr = len([d for d in range(n) if d % self.RING == r])
                            if cntr:
                                e.wait_ge(sems[f"d_{name}_{r}"], 16 * cntr)
                return body
            block.tensor(run("pe"))
            block.scalar(run("act"))
            block.vector(run("dve"))
            block.gpsimd(run("pool"))
            block.sync(run("sp"))


def chunks_of(w):
    if w == 1024:
        return [(0, 512), (512, 512)]
    a = (w + 2) // 3
    r = []
    o = 0
    while o < w:
        n = min(a, w - o)
        r.append((o, n))
        o += n
    return r


class StopBuild(Exception):
    pass


def build_program(debug=(), stop=99):
    nc = bass.Bass("TRN2", target_bir_lowering=False)
    P = Prog(nc, debug)
    P.stop = stop
    try:
        _build_body(nc, P)
    except StopBuild:
        pass
    P.emit()
    return nc, P


def _build_body(nc, P):
    def stage(k):
        if k >= P.stop:
            raise StopBuild()
    din = lambda n, s: P.dram(n, s, F32, kind="ExternalInput")
    xT = din("xT", [D, T + 3])
    vec_d = din("vecT", [128, NV])
    cst_d = din("cst", [128, NCST])
    s5s_d = din("s5s", [128, 192])
    s5b_d = din("s5b", [128, 2, 1024])
    s5c_d = din("s5c", [128, 2, 1024])
    wabd_d = din("wabd", [128, 8, 128])
    wibd_d = din("wibd", [128, 8, 128])
    w_in = din("w_in", [D, 3072])
    w_glu = din("w_glu", [DS5, DS5])
    w_out = din("w_out", [D, D])
    w_up = din("w_up", [D, 2 * DFF])
    w_down = din("w_down", [DFF, D])
    yT = P.dram("yT", [D, T], F32, kind="ExternalOutput")
    ib1 = P.dram("ib1", [128, 16]); ob1 = P.dram("ob1", [128 * NCORE, 16])
    ib2 = P.dram("ib2", [128, 64]); ob2 = P.dram("ob2", [128 * NCORE, 64])
    ib3 = P.dram("ib3", [128, 32]); ob3 = P.dram("ob3", [128 * NCORE, 32])

    PS = [P.psum(f"ps{i}") for i in range(8)]

    VEC = P.sb("vec", [128, NV], F32)
    CST = P.sb("cst", [128, NCST], F32)
    ONES = P.sb("ones", [128, 128], BF16)
    RSTD_RG = P.sb("rstdrg", [128, T], F32)
    RSTD_S5 = P.sb("rstds5", [128, T], F32)
    SM = P.sb("small", [128, 256], F32)
    EPSC = SM[:, 0:1]
    MIXRG = P.sb("mixrg", [128, 8, T], BF16)
    UB = P.sb("ub", [128, 8, T], BF16)
    m_mixer = P.mark()

    def V(name, k, n=1):
        o = VOFF[name] + k
        return VEC[:, o:o + n]
    IDENT = CST[:, C_ID:C_ID + 128]
    BDM = CST[:, C_BD:C_BD + 128]
    IOTA = CST[:, C_IOTA:C_IOTA + 128]
    MTOP = CST[:, C_M:C_M + 1]; MBOT = CST[:, C_M + 1:C_M + 2]
    MEVEN = CST[:, C_M + 2:C_M + 3]; MODD = CST[:, C_M + 3:C_M + 4]
    OHS = CST[:, C_OHS:C_OHS + 8]; OHP = CST[:, C_OHP:C_OHP + 8]
    M96 = CST[:, C_M96:C_M96 + 1]

    P.dma("sp", VEC[:, :], vec_d.ap())
    P.dma("sp", CST[:, :], cst_d.ap())
    P.memset("dve", ONES[:, :], 1.0)
    P.memset("dve", SM[:, :], 0.0)
    P.memset("dve", EPSC, EPS)

    def stats_rstd(ps_list, chunks, out_rb, nfeat):
        for (c0, cn), ps in zip(chunks, ps_list):
            P.act(out_rb[:, c0:c0 + cn], ps[:, 0:cn], AF.Sqrt, bias=EPSC, scale=1.0 / nfeat)
        P.op("dve", lambda e: e.reciprocal(out=out_rb, in_=out_rb), [out_rb], [out_rb])

    XN = P.sb("xn", [128, 16, T + 3], BF16)
    m_xn = P.mark()
    X = P.sb("x", [128, 16, T + 3], F32)
    SQ = [P.sb(f"sq{i}", [128, T + 3], BF16) for i in range(2)]
    RB = P.sb("rb", [128, T + 3], F32)
    xv = xT.ap().rearrange("(k p) t -> p k t", p=128)
    for q in range(4):
        P.dma("sp", X[:, 4 * q:4 * q + 4, :], xv[:, 4 * q:4 * q + 4, :])
    ch3 = chunks_of(T + 3)
    for k in range(16):
        sq = SQ[k % 2]
        if k % 2 == 0:
            P.act(sq[:, :], X[:, k, :], AF.Square)
        else:
            P.tt("pool", sq[:, :], X[:, k, :], X[:, k, :], ALU.mult)
        for ci, (c0, cn) in enumerate(ch3):
            P.mm(PS[ci][:, 0:cn], ONES[:, :], sq[:, c0:c0 + cn], k == 0, k == 15)
    stats_rstd(PS[0:3], ch3, RB[:, :], D)
    for k in range(16):
        P.stt("dve" if k % 2 == 0 else "pool", XN[:, k, :], X[:, k, :], V("n1g", k), RB[:, :], ALU.mult, ALU.mult)
    P.dump("xn", XN[:, 0, :], BF16)
    stage(1)
    P.release(m_xn)

    def load_w(dst, wd, r0, kt, c0, ncol=128):
        src = wd.ap()[r0:r0 + 128 * kt, c0:c0 + ncol].rearrange("(k p) n -> p k n", p=128)
        P.dma("pool", dst, src)

    WT = [P.sb(f"wt{i}", [128, 16, 128], BF16) for i in range(3)]
    wti = [0]

    def next_wt():
        w = WT[wti[0] % 3]
        wti[0] += 1
        return w
    ch2 = chunks_of(T)
    for j in range(8):
        w = next_wt()
        load_w(w[:, :, :], w_in, 0, 16, 2048 + 128 * j)
        for k in range(16):
            for ci, (c0, cn) in enumerate(ch2):
                P.mm(PS[ci][:, 0:cn], w[:, k, :], XN[:, k, 3 + c0:3 + c0 + cn], k == 0, k == 15)
        for ci, (c0, cn) in enumerate(ch2):
            P.copy("act" if ci == 0 else "dve", UB[:, j, c0:c0 + cn], PS[ci][:, 0:cn])
    P.dump("ub", UB[:, 0, :], BF16)
    stage(2)

    AB = P.sb("ab", [128, 16, T], F32)
    WABD = P.sb("wabd", [128, 8, 128], BF16)
    WIBD = P.sb("wibd", [128, 8, 128], BF16)
    CP = P.sb("cp", [128, 16], F32)
    HLPT = P.sb("hlpt", [128, 16], F32)
    SR2 = P.sb("sr2", [128, 16], F32)
    m_rg = P.mark()
    UP = P.sb("up", [128, T + 3], F32)
    U = P.sb("u", [128, T], F32)
    UBF = P.sb("ubf", [128, T], BF16)
    Rg = P.sb("rg", [128, T], F32)
    IG = P.sb("ig", [128, T], F32)
    E2 = P.sb("e2", [128, T], F32)
    HS = P.sb("hs", [128, T], F32)
    P.dma("pool", WABD[:, :, :], wabd_d.ap())
    P.dma("pool", WIBD[:, :, :], wibd_d.ap())
    P.act(CP[:, 0:8], V("lam", 0, 8), AF.Exp, scale=-1.0)
    P.act(CP[:, 0:8], CP[:, 0:8], AF.Ln, bias=1.0)
    P.ts("dve", CP[:, 8:16], CP[:, 0:8], -16.0)
    P.ts("dve", CP[:, 0:8], CP[:, 0:8], -8.0)
    P.memset("dve", SR2[:, :], 0.0)
    for i in range(8):
        w = next_wt()
        load_w(w[:, :, :], w_in, 0, 16, 128 * i)
        for k in range(16):
            for ci, (c0, cn) in enumerate(ch3):
                P.mm(PS[ci][:, 0:cn], w[:, k, :], XN[:, k, c0:c0 + cn], k == 0, k == 15)
        for ci, (c0, cn) in enumerate(ch3):
            P.copy("act" if ci != 1 else "dve", UP[:, c0:c0 + cn], PS[ci][:, 0:cn])
        P.ts("dve", U[:, :], UP[:, 3:3 + T], V("rgcw", 3 * 8 + i), V("rgcb", i), ALU.mult, ALU.add)
        for tap in range(3):
            P.stt("pool" if tap == 1 else "dve", U[:, :], UP[:, tap:tap + T], V("rgcw", tap * 8 + i), U[:, :], ALU.mult, ALU.add)
        P.copy("act", UBF[:, :], U[:, :])
        for ci, (c0, cn) in enumerate(ch2):
            P.mm(PS[3 + ci][:, 0:cn], WABD[:, i, :], UBF[:, c0:c0 + cn], True, True)
            P.mm(PS[5 + ci][:, 0:cn], WIBD[:, i, :], UBF[:, c0:c0 + cn], True, True)
        for ci, (c0, cn) in enumerate(ch2):
            P.act(Rg[:, c0:c0 + cn], PS[3 + ci][:, 0:cn], AF.Sigmoid, bias=V("ba", i))
            P.act(IG[:, c0:c0 + cn], PS[5 + ci][:, 0:cn], AF.Sigmoid, bias=V("bi", i))
        A_i = AB[:, i, :]
        B_i = AB[:, 8 + i, :]
        P.act(A_i, Rg[:, :], AF.Exp, scale=CP[:, i:i + 1])
        P.act(E2[:, :], Rg[:, :], AF.Exp, scale=CP[:, 8 + i:9 + i])
        P.act(E2[:, :], E2[:, :], AF.Sqrt, bias=1.0, scale=-1.0)
        P.tt("pool", B_i, IG[:, :], U[:, :], ALU.mult)
        P.tt("pool", B_i, B_i, E2[:, :], ALU.mult)
        P.scan(HS[:, :], A_i, B_i, 0.0)
        P.copy("dve", HLPT[:, i:i + 1], HS[:, T - 1:T])
        _sr = SR2[:, 2 * i:2 * i + 1]
        _rg = Rg[:, :]
        P.op("dve", (lambda o, i_: (lambda e: e.reduce_sum(out=o, in_=i_, axis=mybir.AxisListType.X)))(_sr, _rg), [_rg], [_sr])
        P.act(HLPT[:, 8 + i:9 + i], SR2[:, 2 * i:2 * i + 1], AF.Exp, scale=CP[:, i:i + 1])
        if i == 0:
            P.dump("u0", U[:, :])
            P.dump("a0", A_i)
            P.dump("b0", B_i)
    stage(3)
    GA1 = P.sb("ga1", [128, NCORE, 16], F32)
    HINA = P.sb("hina", [128, NCORE + 1, 8], F32)
    HINRG = P.sb("hinrg", [128, 8], F32)
    TMP8 = P.sb("tmp8", [128, 8], F32)
    P.dma("sp", ib1.ap(), HLPT[:, :])
    P.collective(ib1, ob1)
    P.dma("sp", GA1[:, :, :], ob1.ap().rearrange("(r p) f -> p r f", p=128))
    P.memset("dve", HINA[:, 0, :], 0.0)
    for r in range(NCORE):
        P.tt("dve", TMP8[:, :], GA1[:, r, 8:16], HINA[:, r, :], ALU.mult)
        P.tt("dve", HINA[:, r + 1, :], TMP8[:, :], GA1[:, r, 0:8], ALU.add)
    P.memset("dve", HINRG[:, :], 0.0)
    for r in range(NCORE):
        P.stt("dve", HINRG[:, :], HINA[:, r, :], OHS[:, r:r + 1], HINRG[:, :], ALU.mult, ALU.add)
    stage(4)
    GG = UP
    for i in range(8):
        A_i = AB[:, i, :]
        B_i = AB[:, 8 + i, :]
        P.scan(HS[:, :], A_i, B_i, HINRG[:, i:i + 1])
        w = next_wt()
        load_w(w[:, :, :], w_in, 0, 16, 1024 + 128 * i)
        for k in range(16):
            for ci, (c0, cn) in enumerate(ch2):
                P.mm(PS[ci][:, 0:cn], w[:, k, :], XN[:, k, 3 + c0:3 + c0 + cn], k == 0, k == 15)
        for ci, (c0, cn) in enumerate(ch2):
            P.act(GG[:, c0:c0 + cn], PS[ci][:, 0:cn], AF.Gelu_apprx_tanh)
        P.tt("dve", U[:, :], GG[:, 0:T], HS[:, :], ALU.mult)
        P.tt("pool", UBF[:, :], U[:, :], U[:, :], ALU.mult)
        for ci, (c0, cn) in enumerate(ch2):
            P.mm(PS[6 + ci][:, 0:cn], ONES[:, :], UBF[:, c0:c0 + cn], i == 0, i == 7)
        P.act(MIXRG[:, i, :], U[:, :], AF.Copy, scale=V("grg", i))
        if i == 0:
            P.dump("yrg0", U[:, :])
    stats_rstd(PS[6:8], ch2, RSTD_RG[:, :], DRG)
    P.release(m_mixer)

    stage(5)
    RS = P.sb("rs", [128, 2, 32, 128], F32)
    off_rs = P.tinfo[RS.name][1]
    TAB = P.sb("tab", [128, 2, 32, 128], F32)
    off_tab = P.tinfo[TAB.name][1]
    KLT = P.sb("klt", [128, 8, 8, 128], BF16)
    S5S = P.sb("s5s", [128, 192], F32)
    S5C = P.sb("s5c", [128, 2, 1024], F32)
    APW = P.sb("apw", [128, 2, 9, 64], F32)
    PAIRC = P.sb("pairc", [128, 8, 32], F32)
    HSL = P.sb("hsl", [128, 64], F32)
    GA2 = P.sb("ga2", [128, NCORE, 64], F32)
    HIN2 = P.sb("hin2", [128, NCORE + 1, 64], F32)
    HINS = P.sb("hins", [128, 64], F32)
    m_s5 = P.mark()
    P.dma("sp", S5S[:, :], s5s_d.ap())
    P.dma("sp", S5C[:, :, :], s5c_d.ap())
    S5B = P.sb("s5b", [128, 2, 1024], F32)
    BB = P.sb("bb", [128, 2, 1024], F32)
    TS = [P.sb(f"ts{i}", [128, 64], F32) for i in range(8)]
    TI = P.sb("ti", [128, 64], I32)
    P.dma("sp", S5B[:, :, :], s5b_d.ap())
    ARE = S5S[:, 0:64]; AIM = S5S[:, 64:128]; LDT = S5S[:, 128:192]
    DT_, RE_, TH_ = TS[0], TS[1], TS[2]
    P.act(DT_[:, :], LDT, AF.Exp)
    P.tt("dve", RE_[:, :], ARE, DT_[:, :], ALU.mult)
    P.tt("dve", TH_[:, :], AIM, DT_[:, :], ALU.mult)

    def sincos(ang_ap, out_sin, out_cos, tmpa, tmpb, tmpi, n):
        for (dst, shift) in ((out_sin, 0.0), (out_cos, 0.5 * np.pi)):
            if shift != 0.0:
                P.ts("dve", tmpa, ang_ap, shift, None, ALU.add)
                src = tmpa
            else:
                src = ang_ap
            P.ts("dve", tmpi, src, 1.0 / TWO_PI, None, ALU.mult)
            P.copy("dve", tmpb, tmpi)
            P.stt("dve", tmpb, tmpb, -TWO_PI, src, ALU.mult, ALU.add)
            P.act(dst, tmpb, AF.Sin)

    ANG8 = P.sb("ang8", [128, 64], F32)
    for l in range(9):
        ang, rho, sn, cs = TS[3], TS[4], TS[5], TS[6]
        P.ts("dve", ang[:, :], TH_[:, :], float(l))
        P.act(rho[:, :], RE_[:, :], AF.Exp, scale=float(l))
        sincos(ang[:, :], sn[:, :], cs[:, :], TS[7][:, :], ANG8[:, :], TI[:, :], 64)
        P.tt("dve", APW[:, 0, l, :], rho[:, :], cs[:, :], ALU.mult)
        P.tt("dve", APW[:, 1, l, :], rho[:, :], sn[:, :], ALU.mult)
    P.ts("dve", TS[3][:, :], TH_[:, :], 8.0)
    P.ts("dve", TI[:, :], TS[3][:, :], 1.0 / TWO_PI, None, ALU.mult)
    P.copy("dve", ANG8[:, :], TI[:, :])
    P.stt("dve", ANG8[:, :], ANG8[:, :], -TWO_PI, TS[3][:, :], ALU.mult, ALU.add)
    XR, DEN, KR, KI = TS[3], TS[4], TS[5], TS[6]
    P.ts("dve", XR[:, :], APW[:, 0, 1, :], -1.0, None, ALU.add)
    P.tt("dve", DEN[:, :], ARE, ARE, ALU.mult)
    P.tt("dve", TS[7][:, :], AIM, AIM, ALU.mult)
    P.tt("dve", DEN[:, :], DEN[:, :], TS[7][:, :], ALU.add)
    P.op("dve", lambda e: e.reciprocal(out=DEN[:, :], in_=DEN[:, :]), [DEN[:, :]], [DEN[:, :]])
    P.tt("dve", KR[:, :], XR[:, :], ARE, ALU.mult)
    P.tt("dve", TS[7][:, :], APW[:, 1, 1, :], AIM, ALU.mult)
    P.tt("dve", KR[:, :], KR[:, :], TS[7][:, :], ALU.add)
    P.tt("dve", KR[:, :], KR[:, :], DEN[:, :], ALU.mult)
    P.tt("dve", KI[:, :], APW[:, 1, 1, :], ARE, ALU.mult)
    P.tt("dve", TS[7][:, :], XR[:, :], AIM, ALU.mult)
    P.tt("dve", KI[:, :], KI[:, :], TS[7][:, :], ALU.subtract)
    P.tt("dve", KI[:, :], KI[:, :], DEN[:, :], ALU.mult)
    b3 = lambda ap: ap.rearrange("p (g h) -> p g h", h=16)
    kb = lambda t: t[:, :].unsqueeze(2).to_broadcast([128, 64, 16])
    BT = P.sb("bt", [128, 1024], F32)
    P.tt("dve", b3(BB[:, 0, :]), b3(S5B[:, 0, :]), kb(KR), ALU.mult)
    P.tt("pool", b3(BT[:, :]), b3(S5B[:, 1, :]), kb(KI), ALU.mult)
    P.tt("dve", BB[:, 0, :], BB[:, 0, :], BT[:, :], ALU.subtract)
    P.tt("dve", b3(BB[:, 1, :]), b3(S5B[:, 1, :]), kb(KR), ALU.mult)
    P.tt("pool", b3(BT[:, :]), b3(S5B[:, 0, :]), kb(KI), ALU.mult)
    P.tt("dve", BB[:, 1, :], BB[:, 1, :], BT[:, :], ALU.add)
    def to_pair(dst, src):
        sv = src.rearrange("p (pp e) -> p pp e", e=2)
        P.ts("dve", dst, sv[:, :, 0], MTOP, None, ALU.mult)
        P.stt("dve", dst, sv[:, :, 1], MBOT, dst, ALU.mult, ALU.add)
    to_pair(PAIRC[:, 0, :], ANG8[:, :])
    RHO8 = TS[3]
    P.act(RHO8[:, :], RE_[:, :], AF.Exp, scale=8.0)
    to_pair(PAIRC[:, 1, :], RHO8[:, :])
    SQR, SQI, TQ = TS[4], TS[5], TS[6]
    P.copy("dve", SQR[:, :], APW[:, 0, 8, :])
    P.copy("dve", SQI[:, :], APW[:, 1, 8, :])
    for _ in range(7):
        P.tt("dve", TQ[:, :], SQR[:, :], SQI[:, :], ALU.mult)
        P.tt("dve", SQR[:, :], SQR[:, :], SQR[:, :], ALU.mult)
        P.tt("dve", SQI[:, :], SQI[:, :], SQI[:, :], ALU.mult)
        P.tt("dve", SQR[:, :], SQR[:, :], SQI[:, :], ALU.subtract)
        P.ts("dve", SQI[:, :], TQ[:, :], 2.0)
    to_pair(PAIRC[:, 2, :], SQR[:, :])
    to_pair(PAIRC[:, 3, :], SQI[:, :])
    ANG = RS[:, 0, :, :]
    ANB = RS[:, 1, :, :]
    _m_tib = P.mark()
    TIB = P.sb("tib", [128, 32, 128], I32)
    P.release(_m_tib)
    phib = PAIRC[:, 0, :].unsqueeze(2).to_broadcast([128, 32, 128])
    iob = IOTA.unsqueeze(1).to_broadcast([128, 32, 128])
    P.tt("dve", ANG, phib, iob, ALU.mult)
    for (dst, shift) in ((TAB[:, 1, :, :], 0.0), (TAB[:, 0, :, :], 0.5 * np.pi)):
        if shift != 0.0:
            P.ts("pool", ANG, ANG, shift, None, ALU.add)
        P.ts("dve", TIB[:, :, :], ANG, 1.0 / TWO_PI, None, ALU.mult)
        P.copy("dve", ANB, TIB[:, :, :])
        P.stt("dve", ANB, ANB, -TWO_PI, ANG, ALU.mult, ALU.add)
        P.act(dst, ANB, AF.Sin)
    CR = TAB[:, 0, :, :]; SR = TAB[:, 1, :, :]
    P.dump("cr", CR[:, 0, :]); P.dump("sr", SR[:, 0, :])
    P.dump("apw", APW[:, 0, :, :].rearrange("p l g -> p (l g)"))
    stage(6)
    PRI = P.sb("pri", [128, 2, 8, 128], F32)
    PT1 = P.sb("pt1", [128, 8, 128], F32)
    PSTK = P.sb("pstk", [128, 8, 128], F32)
    CSTK = P.sb("cstk", [128, 128], F32)
    CT1 = P.sb("ct1", [128, 128], F32)
    W1 = P.sb("w1", [128, 2, 8, 128], BF16)
    W1M = P.sb("w1m", [128, 2, 8, 128], BF16)
    RT = [P.sb(f"rt{i}", [128, 512], F32) for i in range(4)]
    GTMP = P.sb("gtmp", [128, 2, 128], F32)
    GL = P.sb("gl", [128, 64], F32)
    RHOB = lambda pp: PAIRC[:, 1, pp:pp + 1].to_broadcast([128, 128])
    for j in range(8):
        g0 = 8 * j
        apr = APW[:, 0, 0:8, g0:g0 + 8].unsqueeze(3).to_broadcast([128, 8, 8, 16])
        api = APW[:, 1, 0:8, g0:g0 + 8].unsqueeze(3).to_broadcast([128, 8, 8, 16])
        bbr = BB[:, 0, 16 * g0:16 * g0 + 128].rearrange("p (q h) -> p q h", h=16).unsqueeze(1).to_broadcast([128, 8, 8, 16])
        bbi = BB[:, 1, 16 * g0:16 * g0 + 128].rearrange("p (q h) -> p q h", h=16).unsqueeze(1).to_broadcast([128, 8, 8, 16])
        v4 = lambda ap: ap.rearrange("p l (q h) -> p l q h", h=16)
        P.tt("dve", v4(PRI[:, 0, :, :]), apr, bbr, ALU.mult)
        P.tt("pool", v4(PT1[:, :, :]), api, bbi, ALU.mult)
        P.tt("dve", PRI[:, 0, :, :], PRI[:, 0, :, :], PT1[:, :, :], ALU.subtract)
        P.tt("dve", v4(PRI[:, 1, :, :]), apr, bbi, ALU.mult)
        P.tt("pool", v4(PT1[:, :, :]), api, bbr, ALU.mult)
        P.tt("dve", PRI[:, 1, :, :], PRI[:, 1, :, :], PT1[:, :, :], ALU.add)
        P.ts("pool", PT1[:, :, :], PRI[:, 0, :, :], MTOP, None, ALU.mult)
        P.stt("dve", PSTK[:, :, :], PRI[:, 1, :, :], MBOT, PT1[:, :, :], ALU.mult, ALU.add)
        P.ts("pool", CT1[:, :], S5C[:, 0, 128 * j:128 * j + 128], MTOP, None, ALU.mult)
        P.ts("dve", CSTK[:, :], S5C[:, 1, 128 * j:128 * j + 128], MBOT, -1.0, ALU.mult, ALU.mult)
        P.tt("dve", CSTK[:, :], CSTK[:, :], CT1[:, :], ALU.add)
        for l in range(8):
            P.mm(PS[l // 4][:, (l % 4) * 128:(l % 4) * 128 + 128], PSTK[:, l, :], CSTK[:, :], True, True)
        P.stt("dve", CT1[:, :], IDENT, V("s5d", j), PS[0][:, 0:128], ALU.mult, ALU.add)
        P.tt("dve", KLT[:, j, 0, :], CT1[:, :], BDM, ALU.mult)
        bdb = BDM.unsqueeze(1).to_broadcast([128, 3, 128])
        P.tt("dve", KLT[:, j, 1:4, :], PS[0][:, 128:512].rearrange("p (l m) -> p l m", m=128), bdb, ALU.mult)
        bdb4 = BDM.unsqueeze(1).to_broadcast([128, 4, 128])
        P.tt("dve", KLT[:, j, 4:8, :], PS[1][:, :].rearrange("p (l m) -> p l m", m=128), bdb4, ALU.mult)
        for x in range(2):
            for l in range(8):
                P.transpose(PS[2 + x][:, l * 64:l * 64 + 64], PRI[0:64, x, l, :], IDENT[0:64, 0:64])
            tv = PS[2 + x][:, :].rearrange("p (l n) -> p l n", n=64)
            P.ts("dve", W1[:, x, :, 0:64], tv, MEVEN, None, ALU.mult)
            P.act(W1[:, x, :, 64:128], tv, AF.Copy, scale=MODD)
            P.ts("pool", W1M[:, x, :, :], W1[:, x, :, :], M96, None, ALU.mult)
        for pp in range(4):
            for x in range(2):
                for s in range(8):
                    if pp < 3:
                        P.mm(PS[4 + x][:, pp * 128:pp * 128 + 128], W1[32 * pp:32 * pp + 32, x, 7 - s, :],
                             UB[32 * pp:32 * pp + 32, j, s::8], s == 0, s == 7)
                    else:
                        P.mm(PS[4 + x][:, pp * 128:pp * 128 + 128], W1M[64:128, x, 7 - s, :],
                             UB[64:128, j, s::8], s == 0, s == 7)
        crj = CR[:, 4 * j:4 * j + 4, :]; srj = SR[:, 4 * j:4 * j + 4, :]
        v3 = lambda ap: ap.rearrange("p (a c) -> p a c", c=128)
        P.tt("dve", v3(RT[0][:, :]), v3(PS[4][:, :]), crj, ALU.mult)
        P.tt("dve", v3(RT[1][:, :]), v3(PS[5][:, :]), srj, ALU.mult)
        P.tt("pool", RS[:, 0, 4 * j:4 * j + 4, :], v3(RT[0][:, :]), v3(RT[1][:, :]), ALU.add)
        P.tt("dve", v3(RT[2][:, :]), v3(PS[5][:, :]), crj, ALU.mult)
        P.tt("dve", v3(RT[3][:, :]), v3(PS[4][:, :]), srj, ALU.mult)
        P.tt("pool", RS[:, 1, 4 * j:4 * j + 4, :], v3(RT[2][:, :]), v3(RT[3][:, :]), ALU.subtract)
        for pp in range(4):
            PP = 4 * j + pp
            for x in range(2):
                P.scan(GTMP[:, x, :], RHOB(PP), RS[:, x, PP, :], 0.0)
                P.copy("act", GL[:, 32 * x + PP:32 * x + PP + 1], GTMP[:, x, 127:128])
        if j == 0:
            P.dump("klt0", KLT[:, 0, :, :].rearrange("p l m -> p (l m)"), BF16)
            P.dump("w1", W1[:, :, :, :].rearrange("p x l m -> p (x l m)"), BF16)
            P.dump("rs0", RS[:, 0, 0, :])
    stage(7)
    c127 = CR[:, :, 127]; s127 = SR[:, :, 127]
    P.tt("dve", TS[0][:, 0:32], c127, GL[:, 0:32], ALU.mult)
    P.tt("dve", TS[1][:, 0:32], s127, GL[:, 32:64], ALU.mult)
    P.tt("dve", HSL[:, 0:32], TS[0][:, 0:32], TS[1][:, 0:32], ALU.subtract)
    P.tt("dve", TS[0][:, 0:32], s127, GL[:, 0:32], ALU.mult)
    P.tt("dve", TS[1][:, 0:32], c127, GL[:, 32:64], ALU.mult)
    P.tt("dve", HSL[:, 32:64], TS[0][:, 0:32], TS[1][:, 0:32], ALU.add)
    P.dma("sp", ib2.ap(), HSL[:, :])
    P.collective(ib2, ob2)
    P.dma("sp", GA2[:, :, :], ob2.ap().rearrange("(r p) f -> p r f", p=128))
    P.memset("dve", HIN2[:, 0, :], 0.0)
    A1r = PAIRC[:, 2, :]; A1i = PAIRC[:, 3, :]
    for r in range(NCORE):
        hr = HIN2[:, r, 0:32]; hi = HIN2[:, r, 32:64]
        P.tt("dve", TS[0][:, 0:32], A1r, hr, ALU.mult)
        P.tt("dve", TS[1][:, 0:32], A1i, hi, ALU.mult)
        P.tt("dve", TS[0][:, 0:32], TS[0][:, 0:32], TS[1][:, 0:32], ALU.subtract)
        P.tt("dve", HIN2[:, r + 1, 0:32], TS[0][:, 0:32], GA2[:, r, 0:32], ALU.add)
        P.tt("dve", TS[0][:, 0:32], A1i, hr, ALU.mult)
        P.tt("dve", TS[1][:, 0:32], A1r, hi, ALU.mult)
        P.tt("dve", TS[0][:, 0:32], TS[0][:, 0:32], TS[1][:, 0:32], ALU.add)
        P.tt("dve", HIN2[:, r + 1, 32:64], TS[0][:, 0:32], GA2[:, r, 32:64], ALU.add)
    P.memset("dve", HINS[:, :], 0.0)
    for r in range(NCORE):
        P.stt("dve", HINS[:, :], HIN2[:, r, :], OHS[:, r:r + 1], HINS[:, :], ALU.mult, ALU.add)
    P.release(m_s5)
    stage(8)
    HB = P.sb("hb", [128, 2, 32, 128], BF16)
    CA = P.sb("ca", [128, 2, 8, 128], F32)
    CT2 = P.sb("ct2", [128, 8, 128], F32)
    W3 = P.sb("w3", [128, 2, 4, 8, 32], BF16)
    W3Z = P.sb("w3z", [128, 2, 8, 64], BF16)
    P.memset("dve", W3Z[:, :, :, :], 0.0)
    GF = P.sb("gf", [128, 2, 4, 128], F32)
    RT2 = [P.sb(f"rtb{i}", [128, 512], F32) for i in range(4)]
    WG = [P.sb(f"wg{i}", [128, 8, 128], BF16) for i in range(2)]
    SG = P.sb("sg", [128, T], F32)
    YS = P.sb("ys", [128, T], F32)
    YQ = P.sb("yq", [128, T], BF16)
    for j in range(8):
        for pp in range(4):
            PP = 4 * j + pp
            for x in range(2):
                P.scan(GF[:, x, pp, :], RHOB(PP), RS[:, x, PP, :], HINS[:, 32 * x + PP:32 * x + PP + 1])
        crj = CR[:, 4 * j:4 * j + 4, 0:127]; srj = SR[:, 4 * j:4 * j + 4, 0:127]
        v3 = lambda ap: ap.rearrange("p (a c) -> p a c", c=128)[:, :, 0:127]
        gr = GF[:, 0, :, 0:127]; gi = GF[:, 1, :, 0:127]
        P.tt("dve", v3(RT2[0][:, :]), gr, crj, ALU.mult)
        P.tt("pool", v3(RT2[1][:, :]), gi, srj, ALU.mult)
        P.tt("dve", HB[:, 0, 4 * j:4 * j + 4, 1:128], v3(RT2[0][:, :]), v3(RT2[1][:, :]), ALU.subtract)
        P.tt("dve", v3(RT2[2][:, :]), gr, srj, ALU.mult)
        P.tt("pool", v3(RT2[3][:, :]), gi, crj, ALU.mult)
        P.tt("dve", HB[:, 1, 4 * j:4 * j + 4, 1:128], v3(RT2[2][:, :]), v3(RT2[3][:, :]), ALU.add)
    P.copy("dve", HB[:, 0, :, 0], HINS[:, 0:32])
    P.copy("dve", HB[:, 1, :, 0], HINS[:, 32:64])
    P.dump("hb0", HB[:, 0, 0, :], BF16)
    stage(9)
    Z = P.sb("z", [128, 8, T], F32, off=off_rs)
    ZB = P.sb("zb", [128, 8, T], BF16, off=off_tab)
    for j in range(8):
        g0 = 8 * j
        cr4 = S5C[:, 0, 128 * j:128 * j + 128].rearrange("p (q h) -> p q h", h=16).unsqueeze(1).to_broadcast([128, 8, 8, 16])
        ci4 = S5C[:, 1, 128 * j:128 * j + 128].rearrange("p (q h) -> p q h", h=16).unsqueeze(1).to_broadcast([128, 8, 8, 16])
        apr = APW[:, 0, 1:9, g0:g0 + 8].unsqueeze(3).to_broadcast([128, 8, 8, 16])
        api = APW[:, 1, 1:9, g0:g0 + 8].unsqueeze(3).to_broadcast([128, 8, 8, 16])
        v4 = lambda ap: ap.rearrange("p l (q h) -> p l q h", h=16)
        P.tt("dve", v4(CA[:, 0, :, :]), apr, cr4, ALU.mult)
        P.tt("pool", v4(CT2[:, :, :]), api, ci4, ALU.mult)
        P.tt("dve", CA[:, 0, :, :], CA[:, 0, :, :], CT2[:, :, :], ALU.subtract)
        P.tt("dve", v4(CA[:, 1, :, :]), api, cr4, ALU.mult)
        P.tt("pool", v4(CT2[:, :, :]), apr, ci4, ALU.mult)
        P.tt("dve", CA[:, 1, :, :], CA[:, 1, :, :], CT2[:, :, :], ALU.add)
        for x, sgn in ((0, 1.0), (1, -1.0)):
            cav = CA[:, x, :, :].rearrange("p l (pp e h) -> p l pp e h", e=2, h=16)
            for e_, msk in ((0, MTOP), (1, MBOT)):
                dst = W3[:, x, :, :, 16 * e_:16 * e_ + 16].rearrange("p pp l h -> p l pp h")
                P.ts("dve" if e_ == 0 else "pool", dst, cav[:, :, :, e_, :], msk, sgn, ALU.mult, ALU.mult)
        for x in range(2):
            P.copy("dve", W3Z[:, x, :, 32:64], W3[:, x, 3, :, :])
        for jp in range(8):
            bank = PS[jp // 4]
            cols = slice((jp % 4) * 128, (jp % 4) * 128 + 128)
            for s in range(jp + 1):
                P.mm(bank[:, cols], KLT[:, j, jp - s, :], UB[:, j, s::8], s == 0, False)
            for pp in range(4):
                PP = 4 * j + pp
                if pp < 3:
                    P.mm(bank[32 * pp:32 * pp + 32, cols], W3[:, 0, pp, jp, :], HB[:, 0, PP, :], False, False)
                    P.mm(bank[32 * pp:32 * pp + 32, cols], W3[:, 1, pp, jp, :], HB[:, 1, PP, :], False, True)
                else:
                    P.mm(bank[64:128, cols], W3Z[:, 0, jp, :], HB[:, 0, PP, :], False, False)
                    P.mm(bank[64:128, cols], W3Z[:, 1, jp, :], HB[:, 1, PP, :], False, True)
        zv = Z[:, j, :].rearrange("p (c jj) -> p jj c", jj=8)
        for hb_ in range(2):
            P.act(zv[:, 4 * hb_:4 * hb_ + 4, :], PS[hb_][:, :].rearrange("p (jj c) -> p jj c", c=128), AF.Gelu_apprx_tanh)
        P.copy("pool", ZB[:, j, :], Z[:, j, :])
        if j == 0:
            P.dump("w3", W3[:, :, :, :, :].rearrange("p x a l m -> p (x a l m)"), BF16)
            P.dump("z0", Z[:, 0, :])
    stage(10)
    MIXS5 = UB
    for m in range(8):
        w = WG[m % 2]
        load_w(w[:, :, :], w_glu, 0, 8, 128 * m)
        for k in range(8):
            for ci, (c0, cn) in enumerate(ch2):
                P.mm(PS[2 + ci][:, 0:cn], w[:, k, :], ZB[:, k, c0:c0 + cn], k == 0, k == 7)
        for ci, (c0, cn) in enumerate(ch2):
            P.act(SG[:, c0:c0 + cn], PS[2 + ci][:, 0:cn], AF.Sigmoid, bias=V("bglu", m))
        P.tt("dve", YS[:, :], Z[:, m, :], SG[:, :], ALU.mult)
        P.tt("pool", YQ[:, :], YS[:, :], YS[:, :], ALU.mult)
        for ci, (c0, cn) in enumerate(ch2):
            P.mm(PS[4 + ci][:, 0:cn], ONES[:, :], YQ[:, c0:c0 + cn], m == 0, m == 7)
        P.act(MIXS5[:, m, :], YS[:, :], AF.Copy, scale=V("gs5", m))
        if m == 0:
            P.dump("ys0", YS[:, :])
    stats_rstd(PS[4:6], ch2, RSTD_S5[:, :], DS5)
    P.release(m_mixer)

    stage(11)
    X2 = P.sb("x2", [128, 16, T], F32)
    XN2 = P.sb("xn2", [128, 16, T + 2], BF16)
    m_ffn = P.mark()
    WO = [P.sb(f"wo{i}", [128, 16, 128], BF16) for i in range(2)]
    TO = [P.sb(f"to{i}", [128, T], F32) for i in range(2)]
    xv2 = xT.ap().rearrange("(k p) t -> p k t", p=128)
    for q in range(4):
        P.dma("sp", X2[:, 4 * q:4 * q + 4, :], xv2[:, 4 * q:4 * q + 4, 3:3 + T])
    for m in range(16):
        w = WO[m % 2]
        load_w(w[:, :, :], w_out, 0, 16, 128 * m)
        for k in range(8):
            for ci, (c0, cn) in enumerate(ch2):
                P.mm(PS[ci][:, 0:cn], w[:, k, :], MIXRG[:, k, c0:c0 + cn], k == 0, k == 7)
        for k in range(8):
            for ci, (c0, cn) in enumerate(ch2):
                P.mm(PS[2 + ci][:, 0:cn], w[:, 8 + k, :], MIXS5[:, k, c0:c0 + cn], k == 0, k == 7)
        for ci, (c0, cn) in enumerate(ch2):
            P.tt("dve", TO[0][:, c0:c0 + cn], PS[ci][:, 0:cn], RSTD_RG[:, c0:c0 + cn], ALU.mult)
            P.tt("dve", TO[1][:, c0:c0 + cn], PS[2 + ci][:, 0:cn], RSTD_S5[:, c0:c0 + cn], ALU.mult)
        P.tt("pool", TO[0][:, :], TO[0][:, :], TO[1][:, :], ALU.add)
        P.tt("pool", X2[:, m, :], X2[:, m, :], TO[0][:, :], ALU.add)
    P.dump("xmid0", X2[:, 0, :])
    stage(12)
    HX = P.sb("hx", [128, 16, 2], F32)
    GA3 = P.sb("ga3", [128, NCORE, 32], F32)
    XH = P.sb("xh", [128, 16, 2], F32)
    P.copy("dve", HX[:, :, :], X2[:, :, T - 2:T])
    P.dma("sp", ib3.ap(), HX[:, :, :].rearrange("p k t -> p (k t)"))
    P.collective(ib3, ob3)
    P.dma("sp", GA3[:, :, :], ob3.ap().rearrange("(r p) f -> p r f", p=128))
    xhf = XH[:, :, :].rearrange("p k t -> p (k t)")
    P.memset("dve", xhf, 0.0)
    for r in range(NCORE):
        P.stt("dve", xhf, GA3[:, r, :], OHP[:, r:r + 1], xhf, ALU.mult, ALU.add)
    SQ2 = [P.sb(f"sqb{i}", [128, T], BF16) for i in range(2)]
    SQH = P.sb("sqh", [128, 16, 2], BF16)
    RB2 = P.sb("rb2", [128, T + 2], F32)
    P.tt("dve", SQH[:, :, :], XH[:, :, :], XH[:, :, :], ALU.mult)
    for k in range(16):
        sq = SQ2[k % 2]
        if k % 2 == 0:
            P.act(sq[:, :], X2[:, k, :], AF.Square)
        else:
            P.tt("pool", sq[:, :], X2[:, k, :], X2[:, k, :], ALU.mult)
        for ci, (c0, cn) in enumerate(ch2):
            P.mm(PS[ci][:, 0:cn], ONES[:, :], sq[:, c0:c0 + cn], k == 0, k == 15)
        P.mm(PS[2][:, 0:2], ONES[:, :], SQH[:, k, :], k == 0, k == 15)
    stats_rstd([PS[2], PS[0], PS[1]], [(0, 2), (2, 512), (514, 512)], RB2[:, :], D)
    for k in range(16):
        P.stt("dve", XN2[:, k, 0:2], XH[:, k, :], V("n2g", k), RB2[:, 0:2], ALU.mult, ALU.mult)
        P.stt("dve" if k % 2 == 0 else "pool", XN2[:, k, 2:], X2[:, k, :], V("n2g", k), RB2[:, 2:], ALU.mult, ALU.mult)
    P.dump("xn2", XN2[:, 0, :], BF16)
    P.release(m_ffn)

    stage(13)
    NQ = 12
    off_mix = P.tinfo[MIXRG.name][1]
    ACTT = P.sb("actt", [128, NQ, T], BF16, off=off_mix)
    WU = [P.sb(f"wu{i}", [128, 16, 2, 128], BF16) for i in range(2)]
    WD = [P.sb(f"wd{i}", [128, NQ, 128], BF16, off=off_mix + NQ * T * 2 + i * NQ * 256) for i in range(2)]
    GP = [P.sb(f"gp{i}", [128, T + 2], F32) for i in range(2)]
    VP = [P.sb(f"vp{i}", [128, T + 2], F32) for i in range(2)]
    Gc = [P.sb(f"gc{i}", [128, T], F32) for i in range(2)]
    Vc = [P.sb(f"vc{i}", [128, T], F32) for i in range(2)]
    chf = chunks_of(T + 2)
    wdi = 0
    for qq in range(DFF // 128 // NQ):
        for fl in range(NQ):
            f = qq * NQ + fl
            w = WU[f % 2]
            load_w(w[:, :, 0, :], w_up, 0, 16, 128 * f)
            load_w(w[:, :, 1, :], w_up, 0, 16, DFF + 128 * f)
            gp, vp, gc, vc = GP[f % 2], VP[f % 2], Gc[f % 2], Vc[f % 2]
            for half in range(2):
                for k in range(16):
                    for ci, (c0, cn) in enumerate(chf):
                        P.mm(PS[3 * half + ci][:, 0:cn], w[:, k, half, :], XN2[:, k, c0:c0 + cn], k == 0, k == 15)
            for ci, (c0, cn) in enumerate(chf):
                P.copy("act", gp[:, c0:c0 + cn], PS[ci][:, 0:cn])
                P.copy("dve", vp[:, c0:c0 + cn], PS[3 + ci][:, 0:cn])
            for (src, dst, tile_, e0, e1) in ((gp, gc, f, "pool", "dve"), (vp, vc, 48 + f, "dve", "pool")):
                P.ts(e0, dst[:, :], src[:, 2:2 + T], V("fcw", 2 * 96 + tile_), V("fcb", tile_), ALU.mult, ALU.add)
                P.stt(e1, dst[:, :], src[:, 1:1 + T], V("fcw", 96 + tile_), dst[:, :], ALU.mult, ALU.add)
                P.stt(e0, dst[:, :], src[:, 0:T], V("fcw", tile_), dst[:, :], ALU.mult, ALU.add)
            P.act(gc[:, :], gc[:, :], AF.Gelu_apprx_tanh)
            P.tt("pool", ACTT[:, fl, :], gc[:, :], vc[:, :], ALU.mult)
            if f == 0:
                P.dump("act0", ACTT[:, 0, :], BF16)
        for m in range(16):
            w = WD[wdi % 2]
            wdi += 1
            src = w_down.ap()[128 * NQ * qq:128 * NQ * (qq + 1), 128 * m:128 * m + 128].rearrange("(f p) n -> p f n", p=128)
            P.dma("pool", w[:, :, :], src)
            for fl in range(NQ):
                for ci, (c0, cn) in enumerate(ch2):
                    P.mm(PS[6 + ci][:, 0:cn], w[:, fl, :], ACTT[:, fl, c0:c0 + cn], fl == 0, fl == NQ - 1)
            for ci, (c0, cn) in enumerate(ch2):
                P.tt("dve", X2[:, m, c0:c0 + cn], X2[:, m, c0:c0 + cn], PS[6 + ci][:, 0:cn], ALU.add)
    P.release(m_ffn)
    stage(14)
    SQ3 = [P.sb(f"sqc{i}", [128, T], BF16) for i in range(2)]
    RB3 = P.sb("rb3", [128, T], F32)
    for k in range(16):
        sq = SQ3[k % 2]
        if k % 2 == 0:
            P.act(sq[:, :], X2[:, k, :], AF.Square)
        else:
            P.tt("pool", sq[:, :], X2[:, k, :], X2[:, k, :], ALU.mult)
        for ci, (c0, cn) in enumerate(ch2):
            P.mm(PS[ci][:, 0:cn], ONES[:, :], sq[:, c0:c0 + cn], k == 0, k == 15)
    stats_rstd(PS[0:2], ch2, RB3[:, :], D)
    yv = yT.ap().rearrange("(k p) t -> p k t", p=128)
    for k in range(16):
        P.stt("dve" if k % 2 == 0 else "pool", X2[:, k, :], X2[:, k, :], V("fng", k), RB3[:, :], ALU.mult, ALU.mult)
        if k % 4 == 3:
            P.dma("sp", yv[:, k - 3:k + 1, :], X2[:, k - 3:k + 1, :])
    return


def prep_inputs(inp):
    f = np.float32
    x = np.asarray(inp["x"], f)[0]
    xT_full = np.ascontiguousarray(x.T)
    xpad = np.concatenate([np.zeros((D, 3), f), xT_full], axis=1)

    def colT(v):
        v = np.asarray(v, f).reshape(-1)
        return v.reshape(-1, 128).T
    items = {
        "n1g": inp["norm1_g"][0], "rgcw": inp["rg_conv_w"][0], "rgcb": inp["rg_conv_b"][0],
        "ba": inp["rg_ba"][0], "bi": inp["rg_bi"][0], "lam": inp["rg_lambda"][0],
        "s5d": inp["s5_d"][0], "bglu": inp["s5_b_glu"][0], "grg": inp["out_norm_rg_g"][0],
        "gs5": inp["out_norm_s5_g"][0], "n2g": inp["norm2_g"][0], "fcw": inp["ffn_conv_w"][0],
        "fcb": inp["ffn_conv_b"][0], "fng": inp["final_norm_g"],
    }
    vecT = np.ascontiguousarray(np.concatenate([colT(items[n]) for n, _ in VEC_ITEMS], axis=1)).astype(f)
    assert vecT.shape == (128, NV)
    p = np.arange(128)
    cst0 = np.zeros((128, NCST), f)
    cst0[:, C_ID:C_ID + 128] = np.eye(128, dtype=f)
    cst0[:, C_BD:C_BD + 128] = (p[:, None] // 16 == p[None, :] // 16).astype(f)
    cst0[:, C_IOTA:C_IOTA + 128] = np.arange(1, 129, dtype=f)[None, :]
    cst0[:, C_M] = (p < 64); cst0[:, C_M + 1] = (p >= 64)
    cst0[:, C_M + 2] = ((p // 16) % 2 == 0); cst0[:, C_M + 3] = ((p // 16) % 2 == 1)
    cst0[:, C_M96] = (p >= 96)
    dup = lambda a: np.concatenate([a, a], axis=0)
    s5s = dup(np.concatenate([np.asarray(inp["s5_a_re"][0], f).T, np.asarray(inp["s5_a_im"][0], f).T,
                              np.broadcast_to(np.asarray(inp["s5_log_dt"][0], f)[None, :], (64, 64))], axis=1))
    bre = np.asarray(inp["s5_b_re"][0], f).transpose(1, 0, 2).reshape(64, 1024)
    bim = np.asarray(inp["s5_b_im"][0], f).transpose(1, 0, 2).reshape(64, 1024)
    cre = np.asarray(inp["s5_c_re"][0], f).transpose(2, 0, 1).reshape(64, 1024)
    cim = np.asarray(inp["s5_c_im"][0], f).transpose(2, 0, 1).reshape(64, 1024)
    s5b = dup(np.stack([bre, bim], axis=1))
    s5c = dup(np.stack([cre, cim], axis=1))

    def bd(wh):
        wh = np.asarray(wh, f)
        o = np.zeros((128, 8, 128), f)
        for i in range(8):
            o[0:64, i, 0:64] = wh[2 * i]
            o[64:128, i, 64:128] = wh[2 * i + 1]
        return o
    shared = {
        "vecT": vecT, "s5s": np.ascontiguousarray(s5s), "s5b": np.ascontiguousarray(s5b),
        "s5c": np.ascontiguousarray(s5c), "wabd": bd(inp["rg_wa"][0]), "wibd": bd(inp["rg_wi"][0]),
        "w_in": np.ascontiguousarray(np.asarray(inp["w_in"][0], f)),
        "w_glu": np.ascontiguousarray(np.asarray(inp["s5_w_glu"][0], f)),
        "w_out": np.ascontiguousarray(np.asarray(inp["w_out"][0], f)),
        "w_up": np.ascontiguousarray(np.asarray(inp["ffn_w_up"][0], f)),
        "w_down": np.ascontiguousarray(np.asarray(inp["ffn_w_down"][0], f)),
    }
    maps = []
    for c in range(NCORE):
        cst = cst0.copy()
        cst[:, C_OHS + c] = 1.0
        if c > 0:
            cst[:, C_OHP + c - 1] = 1.0
        m = dict(shared)
        m["cst"] = cst
        m["xT"] = np.ascontiguousarray(xpad[:, T * c:T * c + T + 3])
        maps.append(m)
    return maps


_CACHE = {}


def run(inputs, debug=(), stop=99):
    key = (tuple(sorted(debug)), stop)
    if key not in _CACHE:
        _CACHE[key] = build_program(debug, stop)
    nc, P = _CACHE[key]
    maps = prep_inputs(inputs)
    res = run_bass_kernel_spmd(nc, maps, core_ids=list(range(NCORE)))
    return res, P


def kernel(**inputs):
    res, P = run(inputs)
    outT = np.concatenate([np.asarray(r["yT"], np.float32) for r in res.results], axis=1)
    return np.ascontiguousarray(outT.T)[None, :, :].astype(np.float32)
```

```python
import numpy as np
import concourse.bass as bass
import concourse.mybir as mybir
from concourse.bass_utils import run_bass_kernel_spmd

F32 = mybir.dt.float32
BF16 = mybir.dt.bfloat16
I32 = mybir.dt.int32
AF = mybir.ActivationFunctionType
ALU = mybir.AluOpType

NCORE = 8
D = 2048
T = 1024
DRG = 1024
DS5 = 1024
DFF = 6144
EPS = 1e-6
TWO_PI = 6.283185307179586
SB_BASE = 16640
SB_END = 229376

VEC_ITEMS = [("n1g", 16), ("rgcw", 32), ("rgcb", 8), ("ba", 8), ("bi", 8), ("lam", 8),
             ("s5d", 8), ("bglu", 8), ("grg", 8), ("gs5", 8), ("n2g", 16),
             ("fcw", 288), ("fcb", 96), ("fng", 16)]
VOFF = {}
_o = 0
for _n, _c in VEC_ITEMS:
    VOFF[_n] = _o
    _o += _c
NV = _o
C_ID, C_BD, C_IOTA, C_M, C_OHS, C_OHP, C_M96, NCST = 0, 128, 256, 384, 388, 396, 404, 405


class Prog:
    def __init__(self, nc, debug=()):
        self.nc = nc
        self.debug = set(debug)
        self.dumps = []
        self.ops = {e: [] for e in ("pe", "act", "dve", "pool", "sp")}
        self.cnt = {e: 0 for e in ("pe", "act", "dve", "pool")}
        self.sems = {}
        self.W = {}
        self.R = {}
        self.waited = {e: {} for e in self.ops}
        self.tinfo = {}
        self.dma_n = {"sp": 0, "pool": 0}
        self.RING = {"sp": 6, "pool": 3}
        self.sb_top = SB_BASE
        self.cc_n = 0
        self.nps = 0

    def sb(self, name, shape, dtype, off=None):
        esz = 2 if dtype == BF16 else 4
        nbytes = int(np.prod(shape[1:])) * esz
        if off is None:
            off = (self.sb_top + 31) // 32 * 32
            self.sb_top = off + nbytes
            assert self.sb_top <= SB_END, f"SBUF overflow at {name}: {self.sb_top}"
        else:
            assert off + nbytes <= SB_END, f"SBUF overflow at {name}"
        t = self.nc.alloc_sbuf_tensor_at(f"{name}_{len(self.tinfo)}", list(shape), dtype, offset=off)
        self.tinfo[t.name] = ("sb", off, nbytes, esz)
        return t

    def mark(self):
        return self.sb_top

    def release(self, m):
        self.sb_top = m

    def psum(self, name):
        t = self.nc.alloc_psum_tensor(name, [128, 512], F32)
        self.tinfo[t.name] = ("ps", self.nps * 2048, 2048, 4)
        self.nps += 1
        return t

    def dram(self, name, shape, dtype=F32, kind=None):
        if kind is None:
            t = self.nc.dram_tensor(name, list(shape), dtype)
        else:
            t = self.nc.dram_tensor(name, list(shape), dtype, kind=kind)
        self.tinfo[t.name] = ("dr", name, 0, 4)
        return t

    def blocks(self, ap):
        info = self.tinfo[ap.tensor.name]
        if info[0] == "dr":
            return [("dr", info[1])]
        space, base, nbytes, _ = info
        esz = 2 if ap.dtype == BF16 else 4
        tfree = nbytes // esz
        aps = ap.ap
        off = ap.offset % tfree
        hi = off
        for (st, cn) in aps[1:]:
            if st > 0:
                hi += st * (cn - 1)
        lo_b = base + off * esz
        hi_b = base + (hi + 1) * esz
        return [(space, b) for b in range(lo_b // 256, (hi_b - 1) // 256 + 1)]

    def sem(self, name):
        if name not in self.sems:
            self.sems[name] = self.nc.alloc_semaphore(name)
        return self.sems[name]

    def _record(self, eng, fn, reads, writes, kind):
        waits = {}

        def need(tok):
            if tok is not None:
                s, v = tok
                if waits.get(s, 0) < v:
                    waits[s] = v
        rb = [b for ap in reads for b in self.blocks(ap)]
        wb = [b for ap in writes for b in self.blocks(ap)]
        for b in rb:
            need(self.W.get(b))
        for b in wb:
            need(self.W.get(b))
            for s, v in self.R.get(b, {}).items():
                need((s, v))
        if kind == "compute":
            self.cnt[eng] += 1
            tok = ("c_" + eng, self.cnt[eng])
            inc = 1
        elif kind == "dma":
            n = self.dma_n[eng]
            self.dma_n[eng] += 1
            RG_ = self.RING[eng]
            sname = f"d_{eng}_{n % RG_}"
            tok = (sname, 16 * (n // RG_ + 1))
            if n >= RG_:
                need((sname, 16 * (n // RG_)))
            inc = 16
        else:
            self.cc_n += 1
            tok = (f"cc{self.cc_n}", 1)
            inc = 0
        if eng == "pe":
            waits.pop("c_pe", None)
        wl = []
        for s, v in waits.items():
            if self.waited[eng].get(s, 0) < v:
                self.waited[eng][s] = v
                wl.append((s, v))
        for b in rb:
            d = self.R.setdefault(b, {})
            if d.get(tok[0], 0) < tok[1]:
                d[tok[0]] = tok[1]
        for b in wb:
            self.W[b] = tok
            self.R[b] = {}
        for s, _ in wl:
            self.sem(s)
        self.sem(tok[0])
        self.ops[eng].append((wl, fn, tok[0], inc))

    def op(self, eng, fn, reads, writes):
        self._record(eng, fn, reads, writes, "compute")

    def dma(self, eng, out, in_, **kw):
        self._record(eng, lambda e: e.dma_start(out=out, in_=in_, **kw), [in_], [out], "dma")

    def collective(self, ib, ob):
        def fn(e):
            return e.collective_compute("AllGather", ALU.bypass, replica_groups=[list(range(NCORE))],
                                        ins=[ib.ap().opt()], outs=[ob.ap().opt()])
        self._record("pool", fn, [ib.ap()], [ob.ap()], "cc")

    def dump(self, name, ap, dtype=F32):
        if name not in self.debug:
            return
        shape = list(ap.shape)
        t = self.dram("dbg_" + name, shape, dtype, kind="ExternalOutput")
        self.dumps.append("dbg_" + name)
        self.dma("sp", t.ap(), ap)

    def act(self, out, in_, func, bias=None, scale=1.0, accum_out=None, extra_reads=()):
        reads = [in_] + list(extra_reads)
        if not isinstance(bias, (int, float, type(None))):
            reads.append(bias)
        if not isinstance(scale, (int, float)):
            reads.append(scale)
        writes = [out] + ([accum_out] if accum_out is not None else [])
        kw = {}
        if bias is not None:
            kw["bias"] = bias
        if accum_out is not None:
            kw["accum_out"] = accum_out
        self.op("act", lambda e: e.activation(out=out, in_=in_, func=func, scale=scale, **kw), reads, writes)

    def tt(self, eng, out, in0, in1, op):
        self.op(eng, lambda e: e.tensor_tensor(out=out, in0=in0, in1=in1, op=op), [in0, in1], [out])

    def ts(self, eng, out, in0, s1, s2=None, op0=ALU.mult, op1=None):
        reads = [in0] + [s for s in (s1, s2) if not isinstance(s, (int, float, type(None)))]
        if op1 is None:
            self.op(eng, lambda e: e.tensor_scalar(out=out, in0=in0, scalar1=s1, scalar2=None, op0=op0), reads, [out])
        else:
            self.op(eng, lambda e: e.tensor_scalar(out=out, in0=in0, scalar1=s1, scalar2=s2, op0=op0, op1=op1), reads, [out])

    def stt(self, eng, out, in0, scalar, in1, op0, op1):
        reads = [in0, in1] + ([] if isinstance(scalar, (int, float)) else [scalar])
        eng = "dve"
        self.op(eng, lambda e: e.scalar_tensor_tensor(out=out, in0=in0, scalar=scalar, in1=in1, op0=op0, op1=op1), reads, [out])

    def copy(self, eng, out, in_):
        if eng == "act":
            self.act(out, in_, AF.Copy)
        else:
            self.op(eng, lambda e: e.tensor_copy(out=out, in_=in_), [in_], [out])

    def scan(self, out, d0, d1, init):
        reads = [d0, d1] + ([] if isinstance(init, (int, float)) else [init])
        self.op("dve", lambda e: e.tensor_tensor_scan(out=out, data0=d0, data1=d1, initial=init, op0=ALU.mult, op1=ALU.add), reads, [out])

    def mm(self, out, lhsT, rhs, start, stop):
        self.op("pe", lambda e: e.matmul(out, lhsT=lhsT, rhs=rhs, start=start, stop=stop, skip_group_check=True), [lhsT, rhs], [out])

    def transpose(self, out, in_, ident):
        self.op("pe", lambda e: e.transpose(out, in_, ident), [in_, ident], [out])

    def memset(self, eng, ap, val):
        self.op(eng, lambda e: e.memset(ap, val), [], [ap])

    def emit(self):
        nc = self.nc
        sems = self.sems
        with nc.Block() as block:
            def run(name):
                def body(e):
                    for (wl, fn, tsem, inc) in self.ops[name]:
                        for s, v in wl:
                            e.wait_ge(sems[s], v)
                        if inc == 0:
                            fn(e).then_inc(sems[tsem])
                        else:
                            fn(e).then_inc(sems[tsem], inc)
                    if name in ("sp", "pool"):
                        n = self.dma_n[name]
                        RG_ = self.RING[name]
                        for r in range(min(n, RG_)):
                            cntr = len([d for d in range(n) if d % RG_ == r])
                            if cntr:
                                e.wait_ge(sems[f"d_{name}_{r}"], 16 * cntr)
                return body
            block.tensor(run("pe"))
            block.scalar(run("act"))
            block.vector(run("dve"))
            block.gpsimd(run("pool"))
            block.sync(run("sp"))


def chunks_of(w):
    if w == 1024:
        return [(0, 512), (512, 512)]
    a = (w + 2) // 3
    r = []
    o = 0
    while o < w:
        n = min(a, w - o)
        r.append((o, n))
        o += n
    return r


class StopBuild(Exception):
    pass


def build_program(debug=(), stop=99):
    nc = bass.Bass("TRN2", target_bir_lowering=False)
    P = Prog(nc, debug)
    P.stop = stop
    try:
        _build_body(nc, P)
    except StopBuild:
        pass
    P.emit()
    return nc, P


def _build_body(nc, P):
    def stage(k):
        if k >= P.stop:
            raise StopBuild()
    din = lambda n, s: P.dram(n, s, F32, kind="ExternalInput")
    xT = din("xT", [D, T + 3])
    vec_d = din("vecT", [128, NV])
    cst_d = din("cst", [128, NCST])
    s5s_d = din("s5s", [128, 192])
    s5b_d = din("s5b", [128, 2, 1024])
    s5c_d = din("s5c", [128, 2, 1024])
    wabd_d = din("wabd", [128, 8, 128])
    wibd_d = din("wibd", [128, 8, 128])
    w_in = din("w_in", [D, 3072])
    w_glu = din("w_glu", [DS5, DS5])
    w_out = din("w_out", [D, D])
    _small = P.stop < 13
    w_up = din("w_up", [D, 256] if _small else [D, 2 * DFF])
    w_down = din("w_down", [256, D] if _small else [DFF, D])
    yT = P.dram("yT", [D, T], F32, kind="ExternalOutput")
    ib1 = P.dram("ib1", [128, 16]); ob1 = P.dram("ob1", [128 * NCORE, 16])
    ib2 = P.dram("ib2", [128, 64]); ob2 = P.dram("ob2", [128 * NCORE, 64])
    ib3 = P.dram("ib3", [128, 32]); ob3 = P.dram("ob3", [128 * NCORE, 32])

    PS = [P.psum(f"ps{i}") for i in range(8)]

    VEC = P.sb("vec", [128, NV], F32)
    CST = P.sb("cst", [128, NCST], F32)
    ONES = P.sb("ones", [128, 128], BF16)
    RSTD_RG = P.sb("rstdrg", [128, T], F32)
    RSTD_S5 = P.sb("rstds5", [128, T], F32)
    SM = P.sb("small", [128, 256], F32)
    EPSC = SM[:, 0:1]
    MIXRG = P.sb("mixrg", [128, 8, T], BF16)
    UB = P.sb("ub", [128, 8, T], BF16)
    m_mixer = P.mark()

    def V(name, k, n=1):
        o = VOFF[name] + k
        return VEC[:, o:o + n]
    IDENT = CST[:, C_ID:C_ID + 128]
    BDM = CST[:, C_BD:C_BD + 128]
    IOTA = CST[:, C_IOTA:C_IOTA + 128]
    MTOP = CST[:, C_M:C_M + 1]; MBOT = CST[:, C_M + 1:C_M + 2]
    MEVEN = CST[:, C_M + 2:C_M + 3]; MODD = CST[:, C_M + 3:C_M + 4]
    OHS = CST[:, C_OHS:C_OHS + 8]; OHP = CST[:, C_OHP:C_OHP + 8]
    M96 = CST[:, C_M96:C_M96 + 1]

    P.dma("sp", VEC[:, :], vec_d.ap())
    P.dma("sp", CST[:, :], cst_d.ap())
    P.memset("dve", ONES[:, :], 1.0)
    P.memset("dve", SM[:, :], 0.0)
    P.memset("dve", EPSC, EPS)

    def stats_rstd(ps_list, chunks, out_rb, nfeat):
        for (c0, cn), ps in zip(chunks, ps_list):
            P.act(out_rb[:, c0:c0 + cn], ps[:, 0:cn], AF.Sqrt, bias=EPSC, scale=1.0 / nfeat)
        P.op("dve", lambda e: e.reciprocal(out=out_rb, in_=out_rb), [out_rb], [out_rb])

    XN = P.sb("xn", [128, 16, T + 3], BF16)
    m_xn = P.mark()
    X = P.sb("x", [128, 16, T + 3], F32)
    SQ = [P.sb(f"sq{i}", [128, T + 3], BF16) for i in range(2)]
    RB = P.sb("rb", [128, T + 3], F32)
    xv = xT.ap().rearrange("(k p) t -> p k t", p=128)
    for q in range(4):
        P.dma("sp", X[:, 4 * q:4 * q + 4, :], xv[:, 4 * q:4 * q + 4, :])
    ch3 = chunks_of(T + 3)
    for k in range(16):
        sq = SQ[k % 2]
        if k % 2 == 0:
            P.act(sq[:, :], X[:, k, :], AF.Square)
        else:
            P.tt("pool", sq[:, :], X[:, k, :], X[:, k, :], ALU.mult)
        for ci, (c0, cn) in enumerate(ch3):
            P.mm(PS[ci][:, 0:cn], ONES[:, :], sq[:, c0:c0 + cn], k == 0, k == 15)
    stats_rstd(PS[0:3], ch3, RB[:, :], D)
    for k in range(16):
        P.stt("dve" if k % 2 == 0 else "pool", XN[:, k, :], X[:, k, :], V("n1g", k), RB[:, :], ALU.mult, ALU.mult)
    P.dump("xn", XN[:, 0, :], BF16)
    stage(1)
    P.release(m_xn)

    def load_w(dst, wd, r0, kt, c0, ncol=128):
        src = wd.ap()[r0:r0 + 128 * kt, c0:c0 + ncol].rearrange("(k p) n -> p k n", p=128)
        P.dma("pool", dst, src)

    WT = [P.sb(f"wt{i}", [128, 16, 128], BF16) for i in range(3)]
    wti = [0]

    def next_wt():
        w = WT[wti[0] % 3]
        wti[0] += 1
        return w
    ch2 = chunks_of(T)
    wq = {}

    def pre(key, col):
        w_ = next_wt()
        load_w(w_[:, :, :], w_in, 0, 16, col)
        wq[key] = w_
    pre(("s", 0), 2048)
    for j in range(8):
        if j + 1 < 8:
            pre(("s", j + 1), 2048 + 128 * (j + 1))
        else:
            pre(("u", 0), 0)
        w = wq[("s", j)]
        for k in range(16):
            for ci, (c0, cn) in enumerate(ch2):
                P.mm(PS[ci][:, 0:cn], w[:, k, :], XN[:, k, 3 + c0:3 + c0 + cn], k == 0, k == 15)
        for ci, (c0, cn) in enumerate(ch2):
            P.copy("act" if ci == 0 else "dve", UB[:, j, c0:c0 + cn], PS[ci][:, 0:cn])
    P.dump("ub", UB[:, 0, :], BF16)
    stage(2)

    AB = P.sb("ab", [128, 16, T], F32)
    WABD = P.sb("wabd", [128, 8, 128], BF16)
    WIBD = P.sb("wibd", [128, 8, 128], BF16)
    CP = P.sb("cp", [128, 16], F32)
    HLPT = P.sb("hlpt", [128, 16], F32)
    SR2 = P.sb("sr2", [128, 16], F32)
    m_rg = P.mark()
    UP = P.sb("up", [128, T + 3], F32)
    U = P.sb("u", [128, T], F32)
    UBF = P.sb("ubf", [128, T], BF16)
    Rg = P.sb("rg", [128, T], F32)
    IG = P.sb("ig", [128, T], F32)
    E2 = P.sb("e2", [128, T], F32)
    HS = P.sb("hs", [128, T], F32)
    P.dma("pool", WABD[:, :, :], wabd_d.ap())
    P.dma("pool", WIBD[:, :, :], wibd_d.ap())
    P.act(CP[:, 0:8], V("lam", 0, 8), AF.Exp, scale=-1.0)
    P.act(CP[:, 0:8], CP[:, 0:8], AF.Ln, bias=1.0)
    P.ts("dve", CP[:, 8:16], CP[:, 0:8], -16.0)
    P.ts("dve", CP[:, 0:8], CP[:, 0:8], -8.0)
    P.memset("dve", SR2[:, :], 0.0)
    for i in range(8):
        if i + 1 < 8:
            pre(("u", i + 1), 128 * (i + 1))
        else:
            pre(("g", 0), 1024)
        w = wq[("u", i)]
        for k in range(16):
            for ci, (c0, cn) in enumerate(ch3):
                P.mm(PS[ci][:, 0:cn], w[:, k, :], XN[:, k, c0:c0 + cn], k == 0, k == 15)
        for ci, (c0, cn) in enumerate(ch3):
            P.copy("act" if ci != 1 else "dve", UP[:, c0:c0 + cn], PS[ci][:, 0:cn])
        P.ts("dve", U[:, :], UP[:, 3:3 + T], V("rgcw", 3 * 8 + i), V("rgcb", i), ALU.mult, ALU.add)
        for tap in range(3):
            P.stt("pool" if tap == 1 else "dve", U[:, :], UP[:, tap:tap + T], V("rgcw", tap * 8 + i), U[:, :], ALU.mult, ALU.add)
        P.copy("act", UBF[:, :], U[:, :])
        for ci, (c0, cn) in enumerate(ch2):
            P.mm(PS[3 + ci][:, 0:cn], WABD[:, i, :], UBF[:, c0:c0 + cn], True, True)
            P.mm(PS[5 + ci][:, 0:cn], WIBD[:, i, :], UBF[:, c0:c0 + cn], True, True)
        for ci, (c0, cn) in enumerate(ch2):
            P.act(Rg[:, c0:c0 + cn], PS[3 + ci][:, 0:cn], AF.Sigmoid, bias=V("ba", i))
            P.act(IG[:, c0:c0 + cn], PS[5 + ci][:, 0:cn], AF.Sigmoid, bias=V("bi", i))
        A_i = AB[:, i, :]
        B_i = AB[:, 8 + i, :]
        P.act(A_i, Rg[:, :], AF.Exp, scale=CP[:, i:i + 1])
        P.act(E2[:, :], Rg[:, :], AF.Exp, scale=CP[:, 8 + i:9 + i])
        P.act(E2[:, :], E2[:, :], AF.Sqrt, bias=1.0, scale=-1.0)
        P.tt("pool", B_i, IG[:, :], U[:, :], ALU.mult)
        P.tt("pool", B_i, B_i, E2[:, :], ALU.mult)
        P.scan(HS[:, :], A_i, B_i, 0.0)
        P.copy("dve", HLPT[:, i:i + 1], HS[:, T - 1:T])
        _sr = SR2[:, 2 * i:2 * i + 1]
        _rg = Rg[:, :]
        P.op("dve", (lambda o, i_: (lambda e: e.reduce_sum(out=o, in_=i_, axis=mybir.AxisListType.X)))(_sr, _rg), [_rg], [_sr])
        P.act(HLPT[:, 8 + i:9 + i], SR2[:, 2 * i:2 * i + 1], AF.Exp, scale=CP[:, i:i + 1])
        if i == 0:
            P.dump("u0", U[:, :])
            P.dump("a0", A_i)
            P.dump("b0", B_i)
    stage(3)
    GA1 = P.sb("ga1", [128, NCORE, 16], F32)
    HINA = P.sb("hina", [128, NCORE + 1, 8], F32)
    HINRG = P.sb("hinrg", [128, 8], F32)
    TMP8 = P.sb("tmp8", [128, 8], F32)
    P.dma("sp", ib1.ap(), HLPT[:, :])
    P.collective(ib1, ob1)
    P.dma("sp", GA1[:, :, :], ob1.ap().rearrange("(r p) f -> p r f", p=128))
    P.memset("dve", HINA[:, 0, :], 0.0)
    for r in range(NCORE):
        P.tt("dve", TMP8[:, :], GA1[:, r, 8:16], HINA[:, r, :], ALU.mult)
        P.tt("dve", HINA[:, r + 1, :], TMP8[:, :], GA1[:, r, 0:8], ALU.add)
    P.memset("dve", HINRG[:, :], 0.0)
    for r in range(NCORE):
        P.stt("dve", HINRG[:, :], HINA[:, r, :], OHS[:, r:r + 1], HINRG[:, :], ALU.mult, ALU.add)
    stage(4)
    GG = UP
    for i in range(8):
        A_i = AB[:, i, :]
        B_i = AB[:, 8 + i, :]
        P.scan(HS[:, :], A_i, B_i, HINRG[:, i:i + 1])
        if i + 1 < 8:
            pre(("g", i + 1), 1024 + 128 * (i + 1))
        w = wq[("g", i)]
        for k in range(16):
            for ci, (c0, cn) in enumerate(ch2):
                P.mm(PS[ci][:, 0:cn], w[:, k, :], XN[:, k, 3 + c0:3 + c0 + cn], k == 0, k == 15)
        for ci, (c0, cn) in enumerate(ch2):
            P.act(GG[:, c0:c0 + cn], PS[ci][:, 0:cn], AF.Gelu_apprx_tanh)
        P.tt("dve", U[:, :], GG[:, 0:T], HS[:, :], ALU.mult)
        P.tt("pool", UBF[:, :], U[:, :], U[:, :], ALU.mult)
        for ci, (c0, cn) in enumerate(ch2):
            P.mm(PS[6 + ci][:, 0:cn], ONES[:, :], UBF[:, c0:c0 + cn], i == 0, i == 7)
        P.act(MIXRG[:, i, :], U[:, :], AF.Copy, scale=V("grg", i))
        if i == 0:
            P.dump("yrg0", U[:, :])
    stats_rstd(PS[6:8], ch2, RSTD_RG[:, :], DRG)
    P.release(m_mixer)

    stage(5)
    RS = P.sb("rs", [128, 2, 32, 128], F32)
    off_rs = P.tinfo[RS.name][1]
    TAB = P.sb("tab", [128, 2, 32, 128], F32)
    off_tab = P.tinfo[TAB.name][1]
    KLT = P.sb("klt", [128, 8, 8, 128], BF16)
    S5S = P.sb("s5s", [128, 192], F32)
    S5C = P.sb("s5c", [128, 2, 1024], F32)
    APW = P.sb("apw", [128, 2, 9, 64], F32)
    PAIRC = P.sb("pairc", [128, 8, 32], F32)
    HSL = P.sb("hsl", [128, 64], F32)
    GA2 = P.sb("ga2", [128, NCORE, 64], F32)
    HIN2 = P.sb("hin2", [128, NCORE + 1, 64], F32)
    HINS = P.sb("hins", [128, 64], F32)
    m_s5 = P.mark()
    P.dma("sp", S5S[:, :], s5s_d.ap())
    P.dma("sp", S5C[:, :, :], s5c_d.ap())
    S5B = P.sb("s5b", [128, 2, 1024], F32)
    BB = P.sb("bb", [128, 2, 1024], F32)
    TS = [P.sb(f"ts{i}", [128, 64], F32) for i in range(8)]
    TI = P.sb("ti", [128, 64], I32)
    P.dma("sp", S5B[:, :, :], s5b_d.ap())
    ARE = S5S[:, 0:64]; AIM = S5S[:, 64:128]; LDT = S5S[:, 128:192]
    DT_, RE_, TH_ = TS[0], TS[1], TS[2]
    P.act(DT_[:, :], LDT, AF.Exp)
    P.tt("dve", RE_[:, :], ARE, DT_[:, :], ALU.mult)
    P.tt("dve", TH_[:, :], AIM, DT_[:, :], ALU.mult)

    def sincos(ang_ap, out_sin, out_cos, tmpa, tmpb, tmpi, n):
        for (dst, shift) in ((out_sin, 0.0), (out_cos, 0.5 * np.pi)):
            if shift != 0.0:
                P.ts("dve", tmpa, ang_ap, shift, None, ALU.add)
                src = tmpa
            else:
                src = ang_ap
            P.ts("dve", tmpi, src, 1.0 / TWO_PI, None, ALU.mult)
            P.copy("dve", tmpb, tmpi)
            P.stt("dve", tmpb, tmpb, -TWO_PI, src, ALU.mult, ALU.add)
            P.act(dst, tmpb, AF.Sin)

    ANG8 = P.sb("ang8", [128, 64], F32)
    for l in range(9):
        ang, rho, sn, cs = TS[3], TS[4], TS[5], TS[6]
        P.ts("dve", ang[:, :], TH_[:, :], float(l))
        P.act(rho[:, :], RE_[:, :], AF.Exp, scale=float(l))
        sincos(ang[:, :], sn[:, :], cs[:, :], TS[7][:, :], ANG8[:, :], TI[:, :], 64)
        P.tt("dve", APW[:, 0, l, :], rho[:, :], cs[:, :], ALU.mult)
        P.tt("dve", APW[:, 1, l, :], rho[:, :], sn[:, :], ALU.mult)
    P.ts("dve", TS[3][:, :], TH_[:, :], 8.0)
    P.ts("dve", TI[:, :], TS[3][:, :], 1.0 / TWO_PI, None, ALU.mult)
    P.copy("dve", ANG8[:, :], TI[:, :])
    P.stt("dve", ANG8[:, :], ANG8[:, :], -TWO_PI, TS[3][:, :], ALU.mult, ALU.add)
    XR, DEN, KR, KI = TS[3], TS[4], TS[5], TS[6]
    P.ts("dve", XR[:, :], APW[:, 0, 1, :], -1.0, None, ALU.add)
    P.tt("dve", DEN[:, :], ARE, ARE, ALU.mult)
    P.tt("dve", TS[7][:, :], AIM, AIM, ALU.mult)
    P.tt("dve", DEN[:, :], DEN[:, :], TS[7][:, :], ALU.add)
    P.op("dve", lambda e: e.reciprocal(out=DEN[:, :], in_=DEN[:, :]), [DEN[:, :]], [DEN[:, :]])
    P.tt("dve", KR[:, :], XR[:, :], ARE, ALU.mult)
    P.tt("dve", TS[7][:, :], APW[:, 1, 1, :], AIM, ALU.mult)
    P.tt("dve", KR[:, :], KR[:, :], TS[7][:, :], ALU.add)
    P.tt("dve", KR[:, :], KR[:, :], DEN[:, :], ALU.mult)
    P.tt("dve", KI[:, :], APW[:, 1, 1, :], ARE, ALU.mult)
    P.tt("dve", TS[7][:, :], XR[:, :], AIM, ALU.mult)
    P.tt("dve", KI[:, :], KI[:, :], TS[7][:, :], ALU.subtract)
    P.tt("dve", KI[:, :], KI[:, :], DEN[:, :], ALU.mult)
    b3 = lambda ap: ap.rearrange("p (g h) -> p g h", h=16)
    kb = lambda t: t[:, :].unsqueeze(2).to_broadcast([128, 64, 16])
    BT = P.sb("bt", [128, 1024], F32)
    P.tt("dve", b3(BB[:, 0, :]), b3(S5B[:, 0, :]), kb(KR), ALU.mult)
    P.tt("pool", b3(BT[:, :]), b3(S5B[:, 1, :]), kb(KI), ALU.mult)
    P.tt("dve", BB[:, 0, :], BB[:, 0, :], BT[:, :], ALU.subtract)
    P.tt("dve", b3(BB[:, 1, :]), b3(S5B[:, 1, :]), kb(KR), ALU.mult)
    P.tt("pool", b3(BT[:, :]), b3(S5B[:, 0, :]), kb(KI), ALU.mult)
    P.tt("dve", BB[:, 1, :], BB[:, 1, :], BT[:, :], ALU.add)
    def to_pair(dst, src):
        sv = src.rearrange("p (pp e) -> p pp e", e=2)
        P.ts("dve", dst, sv[:, :, 0], MTOP, None, ALU.mult)
        P.stt("dve", dst, sv[:, :, 1], MBOT, dst, ALU.mult, ALU.add)
    to_pair(PAIRC[:, 0, :], ANG8[:, :])
    RHO8 = TS[3]
    P.act(RHO8[:, :], RE_[:, :], AF.Exp, scale=8.0)
    to_pair(PAIRC[:, 1, :], RHO8[:, :])
    SQR, SQI, TQ = TS[4], TS[5], TS[6]
    P.copy("dve", SQR[:, :], APW[:, 0, 8, :])
    P.copy("dve", SQI[:, :], APW[:, 1, 8, :])
    for _ in range(7):
        P.tt("dve", TQ[:, :], SQR[:, :], SQI[:, :], ALU.mult)
        P.tt("dve", SQR[:, :], SQR[:, :], SQR[:, :], ALU.mult)
        P.tt("dve", SQI[:, :], SQI[:, :], SQI[:, :], ALU.mult)
        P.tt("dve", SQR[:, :], SQR[:, :], SQI[:, :], ALU.subtract)
        P.ts("dve", SQI[:, :], TQ[:, :], 2.0)
    to_pair(PAIRC[:, 2, :], SQR[:, :])
    to_pair(PAIRC[:, 3, :], SQI[:, :])
    ANG = RS[:, 0, :, :]
    ANB = RS[:, 1, :, :]
    _m_tib = P.mark()
    TIB = P.sb("tib", [128, 32, 128], I32)
    P.release(_m_tib)
    phib = PAIRC[:, 0, :].unsqueeze(2).to_broadcast([128, 32, 128])
    iob = IOTA.unsqueeze(1).to_broadcast([128, 32, 128])
    P.tt("dve", ANG, phib, iob, ALU.mult)
    for (dst, shift) in ((TAB[:, 1, :, :], 0.0), (TAB[:, 0, :, :], 0.5 * np.pi)):
        if shift != 0.0:
            P.ts("pool", ANG, ANG, shift, None, ALU.add)
        P.ts("dve", TIB[:, :, :], ANG, 1.0 / TWO_PI, None, ALU.mult)
        P.copy("dve", ANB, TIB[:, :, :])
        P.stt("dve", ANB, ANB, -TWO_PI, ANG, ALU.mult, ALU.add)
        P.act(dst, ANB, AF.Sin)
    CR = TAB[:, 0, :, :]; SR = TAB[:, 1, :, :]
    P.dump("cr", CR[:, 0, :]); P.dump("sr", SR[:, 0, :])
    P.dump("apw", APW[:, 0, :, :].rearrange("p l g -> p (l g)"))
    stage(6)
    PRI = P.sb("pri", [128, 2, 8, 128], F32)
    PT1 = P.sb("pt1", [128, 8, 128], F32)
    PSTK = P.sb("pstk", [128, 8, 128], F32)
    CSTK = P.sb("cstk", [128, 128], F32)
    CT1 = P.sb("ct1", [128, 128], F32)
    W1 = P.sb("w1", [128, 2, 8, 128], BF16)
    W1M = P.sb("w1m", [128, 2, 8, 128], BF16)
    RT = [P.sb(f"rt{i}", [128, 512], F32) for i in range(4)]
    GTMP = P.sb("gtmp", [128, 2, 128], F32)
    GL = P.sb("gl", [128, 64], F32)
    RHOB = lambda pp: PAIRC[:, 1, pp:pp + 1].to_broadcast([128, 128])
    for j in range(8):
        g0 = 8 * j
        apr = APW[:, 0, 0:8, g0:g0 + 8].unsqueeze(3).to_broadcast([128, 8, 8, 16])
        api = APW[:, 1, 0:8, g0:g0 + 8].unsqueeze(3).to_broadcast([128, 8, 8, 16])
        bbr = BB[:, 0, 16 * g0:16 * g0 + 128].rearrange("p (q h) -> p q h", h=16).unsqueeze(1).to_broadcast([128, 8, 8, 16])
        bbi = BB[:, 1, 16 * g0:16 * g0 + 128].rearrange("p (q h) -> p q h", h=16).unsqueeze(1).to_broadcast([128, 8, 8, 16])
        v4 = lambda ap: ap.rearrange("p l (q h) -> p l q h", h=16)
        P.tt("dve", v4(PRI[:, 0, :, :]), apr, bbr, ALU.mult)
        P.tt("pool", v4(PT1[:, :, :]), api, bbi, ALU.mult)
        P.tt("dve", PRI[:, 0, :, :], PRI[:, 0, :, :], PT1[:, :, :], ALU.subtract)
        P.tt("dve", v4(PRI[:, 1, :, :]), apr, bbi, ALU.mult)
        P.tt("pool", v4(PT1[:, :, :]), api, bbr, ALU.mult)
        P.tt("dve", PRI[:, 1, :, :], PRI[:, 1, :, :], PT1[:, :, :], ALU.add)
        P.ts("pool", PT1[:, :, :], PRI[:, 0, :, :], MTOP, None, ALU.mult)
        P.stt("dve", PSTK[:, :, :], PRI[:, 1, :, :], MBOT, PT1[:, :, :], ALU.mult, ALU.add)
        P.ts("pool", CT1[:, :], S5C[:, 0, 128 * j:128 * j + 128], MTOP, None, ALU.mult)
        P.ts("dve", CSTK[:, :], S5C[:, 1, 128 * j:128 * j + 128], MBOT, -1.0, ALU.mult, ALU.mult)
        P.tt("dve", CSTK[:, :], CSTK[:, :], CT1[:, :], ALU.add)
        for l in range(8):
            P.mm(PS[l // 4][:, (l % 4) * 128:(l % 4) * 128 + 128], PSTK[:, l, :], CSTK[:, :], True, True)
        P.stt("dve", CT1[:, :], IDENT, V("s5d", j), PS[0][:, 0:128], ALU.mult, ALU.add)
        P.tt("dve", KLT[:, j, 0, :], CT1[:, :], BDM, ALU.mult)
        bdb = BDM.unsqueeze(1).to_broadcast([128, 3, 128])
        P.tt("dve", KLT[:, j, 1:4, :], PS[0][:, 128:512].rearrange("p (l m) -> p l m", m=128), bdb, ALU.mult)
        bdb4 = BDM.unsqueeze(1).to_broadcast([128, 4, 128])
        P.tt("dve", KLT[:, j, 4:8, :], PS[1][:, :].rearrange("p (l m) -> p l m", m=128), bdb4, ALU.mult)
        for x in range(2):
            for l in range(8):
                P.transpose(PS[2 + x][:, l * 64:l * 64 + 64], PRI[0:64, x, l, :], IDENT[0:64, 0:64])
            tv = PS[2 + x][:, :].rearrange("p (l n) -> p l n", n=64)
            P.ts("dve", W1[:, x, :, 0:64], tv, MEVEN, None, ALU.mult)
            P.act(W1[:, x, :, 64:128], tv, AF.Copy, scale=MODD)
            P.ts("pool", W1M[:, x, :, :], W1[:, x, :, :], M96, None, ALU.mult)
        for pp in range(4):
            for x in range(2):
                for s in range(8):
                    if pp < 3:
                        P.mm(PS[4 + x][:, pp * 128:pp * 128 + 128], W1[32 * pp:32 * pp + 32, x, 7 - s, :],
                             UB[32 * pp:32 * pp + 32, j, s::8], s == 0, s == 7)
                    else:
                        P.mm(PS[4 + x][:, pp * 128:pp * 128 + 128], W1M[64:128, x, 7 - s, :],
                             UB[64:128, j, s::8], s == 0, s == 7)
        crj = CR[:, 4 * j:4 * j + 4, :]; srj = SR[:, 4 * j:4 * j + 4, :]
        v3 = lambda ap: ap.rearrange("p (a c) -> p a c", c=128)
        P.tt("dve", v3(RT[0][:, :]), v3(PS[4][:, :]), crj, ALU.mult)
        P.tt("dve", v3(RT[1][:, :]), v3(PS[5][:, :]), srj, ALU.mult)
        P.tt("pool", RS[:, 0, 4 * j:4 * j + 4, :], v3(RT[0][:, :]), v3(RT[1][:, :]), ALU.add)
        P.tt("dve", v3(RT[2][:, :]), v3(PS[5][:, :]), crj, ALU.mult)
        P.tt("dve", v3(RT[3][:, :]), v3(PS[4][:, :]), srj, ALU.mult)
        P.tt("pool", RS[:, 1, 4 * j:4 * j + 4, :], v3(RT[2][:, :]), v3(RT[3][:, :]), ALU.subtract)
        for pp in range(4):
            PP = 4 * j + pp
            for x in range(2):
                P.scan(GTMP[:, x, :], RHOB(PP), RS[:, x, PP, :], 0.0)
                P.copy("act", GL[:, 32 * x + PP:32 * x + PP + 1], GTMP[:, x, 127:128])
        if j == 0:
            P.dump("klt0", KLT[:, 0, :, :].rearrange("p l m -> p (l m)"), BF16)
            P.dump("w1", W1[:, :, :, :].rearrange("p x l m -> p (x l m)"), BF16)
            P.dump("rs0", RS[:, 0, 0, :])
    stage(7)
    c127 = CR[:, :, 127]; s127 = SR[:, :, 127]
    P.tt("dve", TS[0][:, 0:32], c127, GL[:, 0:32], ALU.mult)
    P.tt("dve", TS[1][:, 0:32], s127, GL[:, 32:64], ALU.mult)
    P.tt("dve", HSL[:, 0:32], TS[0][:, 0:32], TS[1][:, 0:32], ALU.subtract)
    P.tt("dve", TS[0][:, 0:32], s127, GL[:, 0:32], ALU.mult)
    P.tt("dve", TS[1][:, 0:32], c127, GL[:, 32:64], ALU.mult)
    P.tt("dve", HSL[:, 32:64], TS[0][:, 0:32], TS[1][:, 0:32], ALU.add)
    P.dma("sp", ib2.ap(), HSL[:, :])
    P.collective(ib2, ob2)
    P.dma("sp", GA2[:, :, :], ob2.ap().rearrange("(r p) f -> p r f", p=128))
    P.memset("dve", HIN2[:, 0, :], 0.0)
    A1r = PAIRC[:, 2, :]; A1i = PAIRC[:, 3, :]
    for r in range(NCORE):
        hr = HIN2[:, r, 0:32]; hi = HIN2[:, r, 32:64]
        P.tt("dve", TS[0][:, 0:32], A1r, hr, ALU.mult)
        P.tt("dve", TS[1][:, 0:32], A1i, hi, ALU.mult)
        P.tt("dve", TS[0][:, 0:32], TS[0][:, 0:32], TS[1][:, 0:32], ALU.subtract)
        P.tt("dve", HIN2[:, r + 1, 0:32], TS[0][:, 0:32], GA2[:, r, 0:32], ALU.add)
        P.tt("dve", TS[0][:, 0:32], A1i, hr, ALU.mult)
        P.tt("dve", TS[1][:, 0:32], A1r, hi, ALU.mult)
        P.tt("dve", TS[0][:, 0:32], TS[0][:, 0:32], TS[1][:, 0:32], ALU.add)
        P.tt("dve", HIN2[:, r + 1, 32:64], TS[0][:, 0:32], GA2[:, r, 32:64], ALU.add)
    P.memset("dve", HINS[:, :], 0.0)
    for r in range(NCORE):
        P.stt("dve", HINS[:, :], HIN2[:, r, :], OHS[:, r:r + 1], HINS[:, :], ALU.mult, ALU.add)
    P.release(m_s5)
    stage(8)
    HB = P.sb("hb", [128, 2, 32, 128], BF16)
    CA = P.sb("ca", [128, 2, 8, 128], F32)
    CT2 = P.sb("ct2", [128, 8, 128], F32)
    W3 = P.sb("w3", [128, 2, 4, 8, 32], BF16)
    W3Z = P.sb("w3z", [128, 2, 8, 64], BF16)
    P.memset("dve", W3Z[:, :, :, :], 0.0)
    GF = P.sb("gf", [128, 2, 4, 128], F32)
    RT2 = [P.sb(f"rtb{i}", [128, 512], F32) for i in range(4)]
    WG = [P.sb(f"wg{i}", [128, 8, 128], BF16) for i in range(2)]
    SG = P.sb("sg", [128, T], F32)
    YS = P.sb("ys", [128, T], F32)
    YQ = P.sb("yq", [128, T], BF16)
    for j in range(8):
        for pp in range(4):
            PP = 4 * j + pp
            for x in range(2):
                P.scan(GF[:, x, pp, :], RHOB(PP), RS[:, x, PP, :], HINS[:, 32 * x + PP:32 * x + PP + 1])
        crj = CR[:, 4 * j:4 * j + 4, 0:127]; srj = SR[:, 4 * j:4 * j + 4, 0:127]
        v3 = lambda ap: ap.rearrange("p (a c) -> p a c", c=128)[:, :, 0:127]
        gr = GF[:, 0, :, 0:127]; gi = GF[:, 1, :, 0:127]
        P.tt("dve", v3(RT2[0][:, :]), gr, crj, ALU.mult)
        P.tt("pool", v3(RT2[1][:, :]), gi, srj, ALU.mult)
        P.tt("dve", HB[:, 0, 4 * j:4 * j + 4, 1:128], v3(RT2[0][:, :]), v3(RT2[1][:, :]), ALU.subtract)
        P.tt("dve", v3(RT2[2][:, :]), gr, srj, ALU.mult)
        P.tt("pool", v3(RT2[3][:, :]), gi, crj, ALU.mult)
        P.tt("dve", HB[:, 1, 4 * j:4 * j + 4, 1:128], v3(RT2[2][:, :]), v3(RT2[3][:, :]), ALU.add)
    P.copy("dve", HB[:, 0, :, 0], HINS[:, 0:32])
    P.copy("dve", HB[:, 1, :, 0], HINS[:, 32:64])
    P.dump("hb0", HB[:, 0, 0, :], BF16)
    stage(9)
    Z = P.sb("z", [128, 8, T], F32, off=off_rs)
    ZB = P.sb("zb", [128, 8, T], BF16, off=off_tab)
    for j in range(8):
        g0 = 8 * j
        cr4 = S5C[:, 0, 128 * j:128 * j + 128].rearrange("p (q h) -> p q h", h=16).unsqueeze(1).to_broadcast([128, 8, 8, 16])
        ci4 = S5C[:, 1, 128 * j:128 * j + 128].rearrange("p (q h) -> p q h", h=16).unsqueeze(1).to_broadcast([128, 8, 8, 16])
        apr = APW[:, 0, 1:9, g0:g0 + 8].unsqueeze(3).to_broadcast([128, 8, 8, 16])
        api = APW[:, 1, 1:9, g0:g0 + 8].unsqueeze(3).to_broadcast([128, 8, 8, 16])
        v4 = lambda ap: ap.rearrange("p l (q h) -> p l q h", h=16)
        P.tt("dve", v4(CA[:, 0, :, :]), apr, cr4, ALU.mult)
        P.tt("pool", v4(CT2[:, :, :]), api, ci4, ALU.mult)
        P.tt("dve", CA[:, 0, :, :], CA[:, 0, :, :], CT2[:, :, :], ALU.subtract)
        P.tt("dve", v4(CA[:, 1, :, :]), api, cr4, ALU.mult)
        P.tt("pool", v4(CT2[:, :, :]), apr, ci4, ALU.mult)
        P.tt("dve", CA[:, 1, :, :], CA[:, 1, :, :], CT2[:, :, :], ALU.add)
        for x, sgn in ((0, 1.0), (1, -1.0)):
            cav = CA[:, x, :, :].rearrange("p l (pp e h) -> p l pp e h", e=2, h=16)
            for e_, msk in ((0, MTOP), (1, MBOT)):
                dst = W3[:, x, :, :, 16 * e_:16 * e_ + 16].rearrange("p pp l h -> p l pp h")
                P.ts("dve" if e_ == 0 else "pool", dst, cav[:, :, :, e_, :], msk, sgn, ALU.mult, ALU.mult)
        for x in range(2):
            P.copy("dve", W3Z[:, x, :, 32:64], W3[:, x, 3, :, :])
        for jp in range(8):
            bank = PS[jp // 4]
            cols = slice((jp % 4) * 128, (jp % 4) * 128 + 128)
            for s in range(jp + 1):
                P.mm(bank[:, cols], KLT[:, j, jp - s, :], UB[:, j, s::8], s == 0, False)
            for pp in range(4):
                PP = 4 * j + pp
                if pp < 3:
                    P.mm(bank[32 * pp:32 * pp + 32, cols], W3[:, 0, pp, jp, :], HB[:, 0, PP, :], False, False)
                    P.mm(bank[32 * pp:32 * pp + 32, cols], W3[:, 1, pp, jp, :], HB[:, 1, PP, :], False, True)
                else:
                    P.mm(bank[64:128, cols], W3Z[:, 0, jp, :], HB[:, 0, PP, :], False, False)
                    P.mm(bank[64:128, cols], W3Z[:, 1, jp, :], HB[:, 1, PP, :], False, True)
        zv = Z[:, j, :].rearrange("p (c jj) -> p jj c", jj=8)
        for hb_ in range(2):
            P.act(zv[:, 4 * hb_:4 * hb_ + 4, :], PS[hb_][:, :].rearrange("p (jj c) -> p jj c", c=128), AF.Gelu_apprx_tanh)
        P.copy("pool", ZB[:, j, :], Z[:, j, :])
        if j == 0:
            P.dump("w3", W3[:, :, :, :, :].rearrange("p x a l m -> p (x a l m)"), BF16)
            P.dump("z0", Z[:, 0, :])
    stage(10)
    MIXS5 = UB
    load_w(WG[0][:, :, :], w_glu, 0, 8, 0)
    for m in range(8):
        w = WG[m % 2]
        if m + 1 < 8:
            load_w(WG[(m + 1) % 2][:, :, :], w_glu, 0, 8, 128 * (m + 1))
        for k in range(8):
            for ci, (c0, cn) in enumerate(ch2):
                P.mm(PS[2 + ci][:, 0:cn], w[:, k, :], ZB[:, k, c0:c0 + cn], k == 0, k == 7)
        for ci, (c0, cn) in enumerate(ch2):
            P.act(SG[:, c0:c0 + cn], PS[2 + ci][:, 0:cn], AF.Sigmoid, bias=V("bglu", m))
        P.tt("dve", YS[:, :], Z[:, m, :], SG[:, :], ALU.mult)
        P.tt("pool", YQ[:, :], YS[:, :], YS[:, :], ALU.mult)
        for ci, (c0, cn) in enumerate(ch2):
            P.mm(PS[4 + ci][:, 0:cn], ONES[:, :], YQ[:, c0:c0 + cn], m == 0, m == 7)
        P.act(MIXS5[:, m, :], YS[:, :], AF.Copy, scale=V("gs5", m))
        if m == 0:
            P.dump("ys0", YS[:, :])
    stats_rstd(PS[4:6], ch2, RSTD_S5[:, :], DS5)
    P.release(m_mixer)

    stage(11)
    X2 = P.sb("x2", [128, 16, T], F32)
    XN2 = P.sb("xn2", [128, 16, T + 2], BF16)
    m_ffn = P.mark()
    WO = [P.sb(f"wo{i}", [128, 16, 128], BF16) for i in range(2)]
    TO = [P.sb(f"to{i}", [128, T], F32) for i in range(2)]
    xv2 = xT.ap().rearrange("(k p) t -> p k t", p=128)
    for q in range(4):
        P.dma("sp", X2[:, 4 * q:4 * q + 4, :], xv2[:, 4 * q:4 * q + 4, 3:3 + T])
    load_w(WO[0][:, :, :], w_out, 0, 16, 0)
    for m in range(16):
        w = WO[m % 2]
        if m + 1 < 16:
            load_w(WO[(m + 1) % 2][:, :, :], w_out, 0, 16, 128 * (m + 1))
        for k in range(8):
            for ci, (c0, cn) in enumerate(ch2):
                P.mm(PS[ci][:, 0:cn], w[:, k, :], MIXRG[:, k, c0:c0 + cn], k == 0, k == 7)
        for k in range(8):
            for ci, (c0, cn) in enumerate(ch2):
                P.mm(PS[2 + ci][:, 0:cn], w[:, 8 + k, :], MIXS5[:, k, c0:c0 + cn], k == 0, k == 7)
        for ci, (c0, cn) in enumerate(ch2):
            P.tt("dve", TO[0][:, c0:c0 + cn], PS[ci][:, 0:cn], RSTD_RG[:, c0:c0 + cn], ALU.mult)
            P.tt("dve", TO[1][:, c0:c0 + cn], PS[2 + ci][:, 0:cn], RSTD_S5[:, c0:c0 + cn], ALU.mult)
        P.tt("pool", TO[0][:, :], TO[0][:, :], TO[1][:, :], ALU.add)
        P.tt("pool", X2[:, m, :], X2[:, m, :], TO[0][:, :], ALU.add)
    P.dump("xmid0", X2[:, 0, :])
    stage(12)
    HX = P.sb("hx", [128, 16, 2], F32)
    GA3 = P.sb("ga3", [128, NCORE, 32], F32)
    XH = P.sb("xh", [128, 16, 2], F32)
    P.copy("dve", HX[:, :, :], X2[:, :, T - 2:T])
    P.dma("sp", ib3.ap(), HX[:, :, :].rearrange("p k t -> p (k t)"))
    P.collective(ib3, ob3)
    P.dma("sp", GA3[:, :, :], ob3.ap().rearrange("(r p) f -> p r f", p=128))
    xhf = XH[:, :, :].rearrange("p k t -> p (k t)")
    P.memset("dve", xhf, 0.0)
    for r in range(NCORE):
        P.stt("dve", xhf, GA3[:, r, :], OHP[:, r:r + 1], xhf, ALU.mult, ALU.add)
    SQ2 = [P.sb(f"sqb{i}", [128, T], BF16) for i in range(2)]
    SQH = P.sb("sqh", [128, 16, 2], BF16)
    RB2 = P.sb("rb2", [128, T + 2], F32)
    P.tt("dve", SQH[:, :, :], XH[:, :, :], XH[:, :, :], ALU.mult)
    for k in range(16):
        sq = SQ2[k % 2]
        if k % 2 == 0:
            P.act(sq[:, :], X2[:, k, :], AF.Square)
        else:
            P.tt("pool", sq[:, :], X2[:, k, :], X2[:, k, :], ALU.mult)
        for ci, (c0, cn) in enumerate(ch2):
            P.mm(PS[ci][:, 0:cn], ONES[:, :], sq[:, c0:c0 + cn], k == 0, k == 15)
        P.mm(PS[2][:, 0:2], ONES[:, :], SQH[:, k, :], k == 0, k == 15)
    stats_rstd([PS[2], PS[0], PS[1]], [(0, 2), (2, 512), (514, 512)], RB2[:, :], D)
    for k in range(16):
        P.stt("dve", XN2[:, k, 0:2], XH[:, k, :], V("n2g", k), RB2[:, 0:2], ALU.mult, ALU.mult)
        P.stt("dve" if k % 2 == 0 else "pool", XN2[:, k, 2:], X2[:, k, :], V("n2g", k), RB2[:, 2:], ALU.mult, ALU.mult)
    P.dump("xn2", XN2[:, 0, :], BF16)
    P.release(m_ffn)

    stage(13)
    NQ = 12
    off_mix = P.tinfo[MIXRG.name][1]
    ACTT = P.sb("actt", [128, NQ, T], BF16, off=off_mix)
    WU = [P.sb(f"wu{i}", [128, 16, 2, 128], BF16) for i in range(2)]
    WD = [P.sb(f"wd{i}", [128, NQ, 128], BF16, off=off_mix + NQ * T * 2 + i * NQ * 256) for i in range(2)]
    GP = [P.sb(f"gp{i}", [128, T + 2], F32) for i in range(2)]
    VP = [P.sb(f"vp{i}", [128, T + 2], F32) for i in range(2)]
    Gc = [P.sb(f"gc{i}", [128, T], F32) for i in range(2)]
    Vc = [P.sb(f"vc{i}", [128, T], F32) for i in range(2)]
    chf = chunks_of(T + 2)
    wdi = 0

    def ld_wu(f_):
        w_ = WU[f_ % 2]
        load_w(w_[:, :, 0, :], w_up, 0, 16, 128 * f_)
        load_w(w_[:, :, 1, :], w_up, 0, 16, DFF + 128 * f_)

    def ld_wd(idx):
        qq_, m_ = idx // 16, idx % 16
        w_ = WD[idx % 2]
        src_ = w_down.ap()[128 * NQ * qq_:128 * NQ * (qq_ + 1), 128 * m_:128 * m_ + 128].rearrange("(f p) n -> p f n", p=128)
        P.dma("pool", w_[:, :, :], src_)
    NFT = DFF // 128
    ld_wu(0)
    for qq in range(NFT // NQ):
        for fl in range(NQ):
            f = qq * NQ + fl
            w = WU[f % 2]
            if f + 1 < NFT:
                ld_wu(f + 1)
            if fl == NQ - 1:
                ld_wd(16 * qq)
            gp, vp, gc, vc = GP[f % 2], VP[f % 2], Gc[f % 2], Vc[f % 2]
            for half in range(2):
                for k in range(16):
                    for ci, (c0, cn) in enumerate(chf):
                        P.mm(PS[3 * half + ci][:, 0:cn], w[:, k, half, :], XN2[:, k, c0:c0 + cn], k == 0, k == 15)
            for ci, (c0, cn) in enumerate(chf):
                P.copy("act", gp[:, c0:c0 + cn], PS[ci][:, 0:cn])
                P.copy("dve", vp[:, c0:c0 + cn], PS[3 + ci][:, 0:cn])
            for (src, dst, tile_, e0, e1) in ((gp, gc, f, "pool", "dve"), (vp, vc, 48 + f, "dve", "pool")):
                P.ts(e0, dst[:, :], src[:, 2:2 + T], V("fcw", 2 * 96 + tile_), V("fcb", tile_), ALU.mult, ALU.add)
                P.stt(e1, dst[:, :], src[:, 1:1 + T], V("fcw", 96 + tile_), dst[:, :], ALU.mult, ALU.add)
                P.stt(e0, dst[:, :], src[:, 0:T], V("fcw", tile_), dst[:, :], ALU.mult, ALU.add)
            P.act(gc[:, :], gc[:, :], AF.Gelu_apprx_tanh)
            P.tt("pool", ACTT[:, fl, :], gc[:, :], vc[:, :], ALU.mult)
            if f == 0:
                P.dump("act0", ACTT[:, 0, :], BF16)
        for m in range(16):
            w = WD[wdi % 2]
            wdi += 1
            if m + 1 < 16:
                ld_wd(16 * qq + m + 1)
            for fl in range(NQ):
                for ci, (c0, cn) in enumerate(ch2):
                    P.mm(PS[6 + ci][:, 0:cn], w[:, fl, :], ACTT[:, fl, c0:c0 + cn], fl == 0, fl == NQ - 1)
            for ci, (c0, cn) in enumerate(ch2):
                P.tt("dve", X2[:, m, c0:c0 + cn], X2[:, m, c0:c0 + cn], PS[6 + ci][:, 0:cn], ALU.add)
    P.release(m_ffn)
    stage(14)
    SQ3 = [P.sb(f"sqc{i}", [128, T], BF16) for i in range(2)]
    RB3 = P.sb("rb3", [128, T], F32)
    for k in range(16):
        sq = SQ3[k % 2]
        if k % 2 == 0:
            P.act(sq[:, :], X2[:, k, :], AF.Square)
        else:
            P.tt("pool", sq[:, :], X2[:, k, :], X2[:, k, :], ALU.mult)
        for ci, (c0, cn) in enumerate(ch2):
            P.mm(PS[ci][:, 0:cn], ONES[:, :], sq[:, c0:c0 + cn], k == 0, k == 15)
    stats_rstd(PS[0:2], ch2, RB3[:, :], D)
    yv = yT.ap().rearrange("(k p) t -> p k t", p=128)
    for k in range(16):
        P.stt("dve" if k % 2 == 0 else "pool", X2[:, k, :], X2[:, k, :], V("fng", k), RB3[:, :], ALU.mult, ALU.mult)
        if k % 4 == 3:
            P.dma("sp", yv[:, k - 3:k + 1, :], X2[:, k - 3:k + 1, :])
    return


def prep_inputs(inp):
    f = np.float32
    x = np.asarray(inp["x"], f)[0]
    xT_full = np.ascontiguousarray(x.T)
    xpad = np.concatenate([np.zeros((D, 3), f), xT_full], axis=1)

    def colT(v):
        v = np.asarray(v, f).reshape(-1)
        return v.reshape(-1, 128).T
    items = {
        "n1g": inp["norm1_g"][0], "rgcw": inp["rg_conv_w"][0], "rgcb": inp["rg_conv_b"][0],
        "ba": inp["rg_ba"][0], "bi": inp["rg_bi"][0], "lam": inp["rg_lambda"][0],
        "s5d": inp["s5_d"][0], "bglu": inp["s5_b_glu"][0], "grg": inp["out_norm_rg_g"][0],
        "gs5": inp["out_norm_s5_g"][0], "n2g": inp["norm2_g"][0], "fcw": inp["ffn_conv_w"][0],
        "fcb": inp["ffn_conv_b"][0], "fng": inp["final_norm_g"],
    }
    vecT = np.ascontiguousarray(np.concatenate([colT(items[n]) for n, _ in VEC_ITEMS], axis=1)).astype(f)
    assert vecT.shape == (128, NV)
    p = np.arange(128)
    cst0 = np.zeros((128, NCST), f)
    cst0[:, C_ID:C_ID + 128] = np.eye(128, dtype=f)
    cst0[:, C_BD:C_BD + 128] = (p[:, None] // 16 == p[None, :] // 16).astype(f)
    cst0[:, C_IOTA:C_IOTA + 128] = np.arange(1, 129, dtype=f)[None, :]
    cst0[:, C_M] = (p < 64); cst0[:, C_M + 1] = (p >= 64)
    cst0[:, C_M + 2] = ((p // 16) % 2 == 0); cst0[:, C_M + 3] = ((p // 16) % 2 == 1)
    cst0[:, C_M96] = (p >= 96)
    dup = lambda a: np.concatenate([a, a], axis=0)
    s5s = dup(np.concatenate([np.asarray(inp["s5_a_re"][0], f).T, np.asarray(inp["s5_a_im"][0], f).T,
                              np.broadcast_to(np.asarray(inp["s5_log_dt"][0], f)[None, :], (64, 64))], axis=1))
    bre = np.asarray(inp["s5_b_re"][0], f).transpose(1, 0, 2).reshape(64, 1024)
    bim = np.asarray(inp["s5_b_im"][0], f).transpose(1, 0, 2).reshape(64, 1024)
    cre = np.asarray(inp["s5_c_re"][0], f).transpose(2, 0, 1).reshape(64, 1024)
    cim = np.asarray(inp["s5_c_im"][0], f).transpose(2, 0, 1).reshape(64, 1024)
    s5b = dup(np.stack([bre, bim], axis=1))
    s5c = dup(np.stack([cre, cim], axis=1))

    def bd(wh):
        wh = np.asarray(wh, f)
        o = np.zeros((128, 8, 128), f)
        for i in range(8):
            o[0:64, i, 0:64] = wh[2 * i]
            o[64:128, i, 64:128] = wh[2 * i + 1]
        return o
    shared = {
        "vecT": vecT, "s5s": np.ascontiguousarray(s5s), "s5b": np.ascontiguousarray(s5b),
        "s5c": np.ascontiguousarray(s5c), "wabd": bd(inp["rg_wa"][0]), "wibd": bd(inp["rg_wi"][0]),
        "w_in": np.ascontiguousarray(np.asarray(inp["w_in"][0], f)),
        "w_glu": np.ascontiguousarray(np.asarray(inp["s5_w_glu"][0], f)),
        "w_out": np.ascontiguousarray(np.asarray(inp["w_out"][0], f)),
        "w_up": np.ascontiguousarray(np.asarray(inp["ffn_w_up"][0], f)),
        "w_down": np.ascontiguousarray(np.asarray(inp["ffn_w_down"][0], f)),
    }
    maps = []
    for c in range(NCORE):
        cst = cst0.copy()
        cst[:, C_OHS + c] = 1.0
        if c > 0:
            cst[:, C_OHP + c - 1] = 1.0
        m = dict(shared)
        m["cst"] = cst
        m["xT"] = np.ascontiguousarray(xpad[:, T * c:T * c + T + 3])
        maps.append(m)
    return maps


_CACHE = {}


def run(inputs, debug=(), stop=99):
    key = (tuple(sorted(debug)), stop)
    if key not in _CACHE:
        _CACHE[key] = build_program(debug, stop)
    nc, P = _CACHE[key]
    maps = prep_inputs(inputs)
    if stop < 13:
        for m in maps:
            m["w_up"] = np.zeros((D, 256), np.float32)
            m["w_down"] = np.zeros((256, D), np.float32)
    res = run_bass_kernel_spmd(nc, maps, core_ids=list(range(NCORE)))
    return res, P


def kernel(**inputs):
    res, P = run(inputs)
    outT = np.concatenate([np.asarray(r["yT"], np.float32) for r in res.results], axis=1)
    return np.ascontiguousarray(outT.T)[None, :, :].astype(np.float32)
```

```python
import numpy as np
import concourse.bass as bass
import concourse.mybir as mybir
from concourse.bass_utils import run_bass_kernel_spmd

F32 = mybir.dt.float32
BF16 = mybir.dt.bfloat16
I32 = mybir.dt.int32
AF = mybir.ActivationFunctionType
ALU = mybir.AluOpType

NCORE = 8
D = 2048
T = 1024
DRG = 1024
DS5 = 1024
DFF = 6144
EPS = 1e-6
TWO_PI = 6.283185307179586
SB_BASE = 16640
SB_END = 229376

VEC_ITEMS = [("n1g", 16), ("rgcw", 32), ("rgcb", 8), ("ba", 8), ("bi", 8), ("lam", 8),
             ("s5d", 8), ("bglu", 8), ("grg", 8), ("gs5", 8), ("n2g", 16),
             ("fcw", 288), ("fcb", 96), ("fng", 16)]
VOFF = {}
_o = 0
for _n, _c in VEC_ITEMS:
    VOFF[_n] = _o
    _o += _c
NV = _o
C_ID, C_BD, C_IOTA, C_M, C_OHS, C_OHP, C_M96, NCST = 0, 128, 256, 384, 388, 396, 404, 405


class Prog:
    def __init__(self, nc, debug=()):
        self.nc = nc
        self.debug = set(debug)
        self.dumps = []
        self.ops = {e: [] for e in ("pe", "act", "dve", "pool", "sp")}
        self.cnt = {e: 0 for e in ("pe", "act", "dve", "pool")}
        self.sems = {}
        self.W = {}
        self.R = {}
        self.waited = {e: {} for e in self.ops}
        self.tinfo = {}
        self.dma_n = {"sp": 0, "pool": 0}
        self.RING = {"sp": 6, "pool": 3}
        self.sb_top = SB_BASE
        self.cc_n = 0
        self.nps = 0

    def sb(self, name, shape, dtype, off=None):
        esz = 2 if dtype == BF16 else 4
        nbytes = int(np.prod(shape[1:])) * esz
        if off is None:
            off = (self.sb_top + 31) // 32 * 32
            self.sb_top = off + nbytes
            assert self.sb_top <= SB_END, f"SBUF overflow at {name}: {self.sb_top}"
        else:
            assert off + nbytes <= SB_END, f"SBUF overflow at {name}"
        t = self.nc.alloc_sbuf_tensor_at(f"{name}_{len(self.tinfo)}", list(shape), dtype, offset=off)
        self.tinfo[t.name] = ("sb", off, nbytes, esz)
        return t

    def mark(self):
        return self.sb_top

    def release(self, m):
        self.sb_top = m

    def psum(self, name):
        t = self.nc.alloc_psum_tensor(name, [128, 512], F32)
        self.tinfo[t.name] = ("ps", self.nps * 2048, 2048, 4)
        self.nps += 1
        return t

    def dram(self, name, shape, dtype=F32, kind=None):
        if kind is None:
            t = self.nc.dram_tensor(name, list(shape), dtype)
        else:
            t = self.nc.dram_tensor(name, list(shape), dtype, kind=kind)
        self.tinfo[t.name] = ("dr", name, 0, 4)
        return t

    def blocks(self, ap):
        info = self.tinfo[ap.tensor.name]
        if info[0] == "dr":
            return [("dr", info[1])]
        space, base, nbytes, _ = info
        esz = 2 if ap.dtype == BF16 else 4
        tfree = nbytes // esz
        aps = ap.ap
        off = ap.offset % tfree
        hi = off
        for (st, cn) in aps[1:]:
            if st > 0:
                hi += st * (cn - 1)
        lo_b = base + off * esz
        hi_b = base + (hi + 1) * esz
        return [(space, b) for b in range(lo_b // 256, (hi_b - 1) // 256 + 1)]

    def sem(self, name):
        if name not in self.sems:
            self.sems[name] = self.nc.alloc_semaphore(name)
        return self.sems[name]

    def _record(self, eng, fn, reads, writes, kind, ms=True):
        waits = {}

        def need(tok):
            if tok is not None:
                s, v = tok
                if waits.get(s, 0) < v:
                    waits[s] = v
        rb = [b for ap in reads for b in self.blocks(ap)]
        wb = [b for ap in writes for b in self.blocks(ap)]
        for b in rb:
            need(self.W.get(b))
        for b in wb:
            need(self.W.get(b))
            for s, v in self.R.get(b, {}).items():
                need((s, v))
        if kind == "compute" and not ms:
            assert eng == "pe"
            tok = ("c_" + eng, self.cnt[eng] + 1)
            inc = None
        elif kind == "compute":
            self.cnt[eng] += 1
            tok = ("c_" + eng, self.cnt[eng])
            inc = 1
        elif kind == "dma":
            n = self.dma_n[eng]
            self.dma_n[eng] += 1
            RG_ = self.RING[eng]
            sname = f"d_{eng}_{n % RG_}"
            tok = (sname, 16 * (n // RG_ + 1))
            if n >= RG_:
                need((sname, 16 * (n // RG_)))
            inc = 16
        else:
            self.cc_n += 1
            tok = (f"cc{self.cc_n}", 1)
            inc = 0
        if eng == "pe":
            waits.pop("c_pe", None)
        wl = []
        for s, v in waits.items():
            if self.waited[eng].get(s, 0) < v:
                self.waited[eng][s] = v
                wl.append((s, v))
        for b in rb:
            d = self.R.setdefault(b, {})
            if d.get(tok[0], 0) < tok[1]:
                d[tok[0]] = tok[1]
        for b in wb:
            self.W[b] = tok
            self.R[b] = {}
        for s, _ in wl:
            self.sem(s)
        self.sem(tok[0])
        self.ops[eng].append((wl, fn, tok[0], inc))

    def op(self, eng, fn, reads, writes, ms=True):
        self._record(eng, fn, reads, writes, "compute", ms=ms)

    def dma(self, eng, out, in_, **kw):
        self._record(eng, lambda e: e.dma_start(out=out, in_=in_, **kw), [in_], [out], "dma")

    def collective(self, ib, ob):
        def fn(e):
            return e.collective_compute("AllGather", ALU.bypass, replica_groups=[list(range(NCORE))],
                                        ins=[ib.ap().opt()], outs=[ob.ap().opt()])
        self._record("pool", fn, [ib.ap()], [ob.ap()], "cc")

    def dump(self, name, ap, dtype=F32):
        if name not in self.debug:
            return
        shape = list(ap.shape)
        t = self.dram("dbg_" + name, shape, dtype, kind="ExternalOutput")
        self.dumps.append("dbg_" + name)
        self.dma("sp", t.ap(), ap)

    def act(self, out, in_, func, bias=None, scale=1.0, accum_out=None, extra_reads=()):
        reads = [in_] + list(extra_reads)
        if not isinstance(bias, (int, float, type(None))):
            reads.append(bias)
        if not isinstance(scale, (int, float)):
            reads.append(scale)
        writes = [out] + ([accum_out] if accum_out is not None else [])
        kw = {}
        if bias is not None:
            kw["bias"] = bias
        if accum_out is not None:
            kw["accum_out"] = accum_out
        self.op("act", lambda e: e.activation(out=out, in_=in_, func=func, scale=scale, **kw), reads, writes)

    def tt(self, eng, out, in0, in1, op):
        self.op(eng, lambda e: e.tensor_tensor(out=out, in0=in0, in1=in1, op=op), [in0, in1], [out])

    def ts(self, eng, out, in0, s1, s2=None, op0=ALU.mult, op1=None):
        reads = [in0] + [s for s in (s1, s2) if not isinstance(s, (int, float, type(None)))]
        if op1 is None:
            self.op(eng, lambda e: e.tensor_scalar(out=out, in0=in0, scalar1=s1, scalar2=None, op0=op0), reads, [out])
        else:
            self.op(eng, lambda e: e.tensor_scalar(out=out, in0=in0, scalar1=s1, scalar2=s2, op0=op0, op1=op1), reads, [out])

    def stt(self, eng, out, in0, scalar, in1, op0, op1):
        reads = [in0, in1] + ([] if isinstance(scalar, (int, float)) else [scalar])
        eng = "dve"
        self.op(eng, lambda e: e.scalar_tensor_tensor(out=out, in0=in0, scalar=scalar, in1=in1, op0=op0, op1=op1), reads, [out])

    def copy(self, eng, out, in_):
        if eng == "act":
            self.act(out, in_, AF.Copy)
        else:
            self.op(eng, lambda e: e.tensor_copy(out=out, in_=in_), [in_], [out])

    def scan(self, out, d0, d1, init):
        reads = [d0, d1] + ([] if isinstance(init, (int, float)) else [init])
        self.op("dve", lambda e: e.tensor_tensor_scan(out=out, data0=d0, data1=d1, initial=init, op0=ALU.mult, op1=ALU.add), reads, [out])

    def mm(self, out, lhsT, rhs, start, stop, ms=False):
        self.op("pe", lambda e: e.matmul(out, lhsT=lhsT, rhs=rhs, start=start, stop=stop, skip_group_check=True), [lhsT, rhs], [out],
                ms=bool(stop or ms))

    def transpose(self, out, in_, ident):
        self.op("pe", lambda e: e.transpose(out, in_, ident), [in_, ident], [out])

    def memset(self, eng, ap, val):
        self.op(eng, lambda e: e.memset(ap, val), [], [ap])

    def emit(self):
        nc = self.nc
        sems = self.sems
        with nc.Block() as block:
            def run(name):
                def body(e):
                    for (wl, fn, tsem, inc) in self.ops[name]:
                        for s, v in wl:
                            e.wait_ge(sems[s], v)
                        if inc is None:
                            fn(e)
                        elif inc == 0:
                            fn(e).then_inc(sems[tsem])
                        else:
                            fn(e).then_inc(sems[tsem], inc)
                    if name in ("sp", "pool"):
                        n = self.dma_n[name]
                        RG_ = self.RING[name]
                        for r in range(min(n, RG_)):
                            cntr = len([d for d in range(n) if d % RG_ == r])
                            if cntr:
                                e.wait_ge(sems[f"d_{name}_{r}"], 16 * cntr)
                return body
            block.tensor(run("pe"))
            block.scalar(run("act"))
            block.vector(run("dve"))
            block.gpsimd(run("pool"))
            block.sync(run("sp"))


def chunks_of(w):
    if w == 1024:
        return [(0, 512), (512, 512)]
    a = (w + 2) // 3
    r = []
    o = 0
    while o < w:
        n = min(a, w - o)
        r.append((o, n))
        o += n
    return r


class StopBuild(Exception):
    pass


def build_program(debug=(), stop=99):
    nc = bass.Bass("TRN2", target_bir_lowering=False)
    P = Prog(nc, debug)
    P.stop = stop
    try:
        _build_body(nc, P)
    except StopBuild:
        pass
    P.emit()
    return nc, P


def _build_body(nc, P):
    def stage(k):
        if k >= P.stop:
            raise StopBuild()
    din = lambda n, s: P.dram(n, s, F32, kind="ExternalInput")
    xT = din("xT", [D, T + 3])
    vec_d = din("vecT", [128, NV])
    cst_d = din("cst", [128, NCST])
    s5s_d = din("s5s", [128, 192])
    s5b_d = din("s5b", [128, 2, 1024])
    s5c_d = din("s5c", [128, 2, 1024])
    wabd_d = din("wabd", [128, 8, 128])
    wibd_d = din("wibd", [128, 8, 128])
    w_in = din("w_in", [24, 128, 16, 128])
    w_glu = din("w_glu", [8, 128, 8, 128])
    w_out = din("w_out", [16, 128, 16, 128])
    _small = P.stop < 13
    w_up = din("w_up", [2 if _small else 96, 128, 16, 128])
    w_down = din("w_down", [2 if _small else 64, 128, 12, 128])
    yT = P.dram("yT", [D, T], F32, kind="ExternalOutput")
    ib1 = P.dram("ib1", [128, 16]); ob1 = P.dram("ob1", [128 * NCORE, 16])
    ib2 = P.dram("ib2", [128, 64]); ob2 = P.dram("ob2", [128 * NCORE, 64])
    ib3 = P.dram("ib3", [128, 32]); ob3 = P.dram("ob3", [128 * NCORE, 32])

    PS = [P.psum(f"ps{i}") for i in range(8)]

    VEC = P.sb("vec", [128, NV], F32)
    CST = P.sb("cst", [128, NCST], F32)
    ONES = P.sb("ones", [128, 128], BF16)
    RSTD_RG = P.sb("rstdrg", [128, T], F32)
    RSTD_S5 = P.sb("rstds5", [128, T], F32)
    SM = P.sb("small", [128, 256], F32)
    EPSC = SM[:, 0:1]
    MIXRG = P.sb("mixrg", [128, 8, T], BF16)
    UB = P.sb("ub", [128, 8, T], BF16)
    m_mixer = P.mark()

    def V(name, k, n=1):
        o = VOFF[name] + k
        return VEC[:, o:o + n]
    IDENT = CST[:, C_ID:C_ID + 128]
    BDM = CST[:, C_BD:C_BD + 128]
    IOTA = CST[:, C_IOTA:C_IOTA + 128]
    MTOP = CST[:, C_M:C_M + 1]; MBOT = CST[:, C_M + 1:C_M + 2]
    MEVEN = CST[:, C_M + 2:C_M + 3]; MODD = CST[:, C_M + 3:C_M + 4]
    OHS = CST[:, C_OHS:C_OHS + 8]; OHP = CST[:, C_OHP:C_OHP + 8]
    M96 = CST[:, C_M96:C_M96 + 1]

    P.dma("sp", VEC[:, :], vec_d.ap())
    P.dma("sp", CST[:, :], cst_d.ap())
    P.memset("dve", ONES[:, :], 1.0)
    P.memset("dve", SM[:, :], 0.0)
    P.memset("dve", EPSC, EPS)

    def stats_rstd(ps_list, chunks, out_rb, nfeat):
        for (c0, cn), ps in zip(chunks, ps_list):
            P.act(out_rb[:, c0:c0 + cn], ps[:, 0:cn], AF.Sqrt, bias=EPSC, scale=1.0 / nfeat)
        P.op("dve", lambda e: e.reciprocal(out=out_rb, in_=out_rb), [out_rb], [out_rb])

    XN = P.sb("xn", [128, 16, T + 3], BF16)
    m_xn = P.mark()
    X = P.sb("x", [128, 16, T + 3], F32)
    SQ = [P.sb(f"sq{i}", [128, T + 3], BF16) for i in range(2)]
    RB = P.sb("rb", [128, T + 3], F32)
    xv = xT.ap().rearrange("(k p) t -> p k t", p=128)
    for q in range(4):
        P.dma("sp", X[:, 4 * q:4 * q + 4, :], xv[:, 4 * q:4 * q + 4, :])
    ch3 = chunks_of(T + 3)
    for k in range(16):
        sq = SQ[k % 2]
        if k % 2 == 0:
            P.act(sq[:, :], X[:, k, :], AF.Square)
        else:
            P.tt("pool", sq[:, :], X[:, k, :], X[:, k, :], ALU.mult)
        for ci, (c0, cn) in enumerate(ch3):
            P.mm(PS[ci][:, 0:cn], ONES[:, :], sq[:, c0:c0 + cn], k == 0, k == 15, ms=True)
    stats_rstd(PS[0:3], ch3, RB[:, :], D)
    for k in range(16):
        P.stt("dve" if k % 2 == 0 else "pool", XN[:, k, :], X[:, k, :], V("n1g", k), RB[:, :], ALU.mult, ALU.mult)
    P.dump("xn", XN[:, 0, :], BF16)
    stage(1)
    P.release(m_xn)

    def load_w(dst, wd, r0, kt, c0, ncol=128):
        src = wd.ap()[c0 // 128]
        P.dma("pool", dst, src, max_dma_last_dim=4096)

    WT = [P.sb(f"wt{i}", [128, 16, 128], BF16) for i in range(3)]
    wti = [0]

    def next_wt():
        w = WT[wti[0] % 3]
        wti[0] += 1
        return w
    ch2 = chunks_of(T)
    wq = {}

    def pre(key, col):
        w_ = next_wt()
        load_w(w_[:, :, :], w_in, 0, 16, col)
        wq[key] = w_
    pre(("s", 0), 2048)
    for j in range(8):
        if j + 1 < 8:
            pre(("s", j + 1), 2048 + 128 * (j + 1))
        else:
            pre(("u", 0), 0)
        w = wq[("s", j)]
        for k in range(16):
            for ci, (c0, cn) in enumerate(ch2):
                P.mm(PS[ci][:, 0:cn], w[:, k, :], XN[:, k, 3 + c0:3 + c0 + cn], k == 0, k == 15)
        for ci, (c0, cn) in enumerate(ch2):
            P.copy("act" if ci == 0 else "dve", UB[:, j, c0:c0 + cn], PS[ci][:, 0:cn])
    P.dump("ub", UB[:, 0, :], BF16)
    stage(2)

    AB = P.sb("ab", [128, 16, T], F32)
    WABD = P.sb("wabd", [128, 8, 128], BF16)
    WIBD = P.sb("wibd", [128, 8, 128], BF16)
    CP = P.sb("cp", [128, 16], F32)
    HLPT = P.sb("hlpt", [128, 16], F32)
    SR2 = P.sb("sr2", [128, 16], F32)
    m_rg = P.mark()
    UP = P.sb("up", [128, T + 3], F32)
    U = P.sb("u", [128, T], F32)
    UBF = P.sb("ubf", [128, T], BF16)
    Rg = P.sb("rg", [128, T], F32)
    IG = P.sb("ig", [128, T], F32)
    E2 = P.sb("e2", [128, T], F32)
    HS = P.sb("hs", [128, T], F32)
    P.dma("pool", WABD[:, :, :], wabd_d.ap())
    P.dma("pool", WIBD[:, :, :], wibd_d.ap())
    P.act(CP[:, 0:8], V("lam", 0, 8), AF.Exp, scale=-1.0)
    P.act(CP[:, 0:8], CP[:, 0:8], AF.Ln, bias=1.0)
    P.ts("dve", CP[:, 8:16], CP[:, 0:8], -16.0)
    P.ts("dve", CP[:, 0:8], CP[:, 0:8], -8.0)
    P.memset("dve", SR2[:, :], 0.0)
    for i in range(8):
        if i + 1 < 8:
            pre(("u", i + 1), 128 * (i + 1))
        else:
            pre(("g", 0), 1024)
        w = wq[("u", i)]
        for k in range(16):
            for ci, (c0, cn) in enumerate(ch3):
                P.mm(PS[ci][:, 0:cn], w[:, k, :], XN[:, k, c0:c0 + cn], k == 0, k == 15)
        for ci, (c0, cn) in enumerate(ch3):
            P.copy("act" if ci != 1 else "dve", UP[:, c0:c0 + cn], PS[ci][:, 0:cn])
        P.ts("dve", U[:, :], UP[:, 3:3 + T], V("rgcw", 3 * 8 + i), V("rgcb", i), ALU.mult, ALU.add)
        for tap in range(3):
            P.stt("pool" if tap == 1 else "dve", U[:, :], UP[:, tap:tap + T], V("rgcw", tap * 8 + i), U[:, :], ALU.mult, ALU.add)
        P.copy("act", UBF[:, :], U[:, :])
        for ci, (c0, cn) in enumerate(ch2):
            P.mm(PS[3 + ci][:, 0:cn], WABD[:, i, :], UBF[:, c0:c0 + cn], True, True)
            P.mm(PS[5 + ci][:, 0:cn], WIBD[:, i, :], UBF[:, c0:c0 + cn], True, True)
        for ci, (c0, cn) in enumerate(ch2):
            P.act(Rg[:, c0:c0 + cn], PS[3 + ci][:, 0:cn], AF.Sigmoid, bias=V("ba", i))
            P.act(IG[:, c0:c0 + cn], PS[5 + ci][:, 0:cn], AF.Sigmoid, bias=V("bi", i))
        A_i = AB[:, i, :]
        B_i = AB[:, 8 + i, :]
        P.act(A_i, Rg[:, :], AF.Exp, scale=CP[:, i:i + 1])
        P.act(E2[:, :], Rg[:, :], AF.Exp, scale=CP[:, 8 + i:9 + i])
        P.act(E2[:, :], E2[:, :], AF.Sqrt, bias=1.0, scale=-1.0)
        P.tt("pool", B_i, IG[:, :], U[:, :], ALU.mult)
        P.tt("pool", B_i, B_i, E2[:, :], ALU.mult)
        P.scan(HS[:, :], A_i, B_i, 0.0)
        P.copy("dve", HLPT[:, i:i + 1], HS[:, T - 1:T])
        _sr = SR2[:, 2 * i:2 * i + 1]
        _rg = Rg[:, :]
        P.op("dve", (lambda o, i_: (lambda e: e.reduce_sum(out=o, in_=i_, axis=mybir.AxisListType.X)))(_sr, _rg), [_rg], [_sr])
        P.act(HLPT[:, 8 + i:9 + i], SR2[:, 2 * i:2 * i + 1], AF.Exp, scale=CP[:, i:i + 1])
        if i == 0:
            P.dump("u0", U[:, :])
            P.dump("a0", A_i)
            P.dump("b0", B_i)
    stage(3)
    GA1 = P.sb("ga1", [128, NCORE, 16], F32)
    HINA = P.sb("hina", [128, NCORE + 1, 8], F32)
    HINRG = P.sb("hinrg", [128, 8], F32)
    TMP8 = P.sb("tmp8", [128, 8], F32)
    P.dma("sp", ib1.ap(), HLPT[:, :])
    P.collective(ib1, ob1)
    P.dma("sp", GA1[:, :, :], ob1.ap().rearrange("(r p) f -> p r f", p=128))
    P.memset("dve", HINA[:, 0, :], 0.0)
    for r in range(NCORE):
        P.tt("dve", TMP8[:, :], GA1[:, r, 8:16], HINA[:, r, :], ALU.mult)
        P.tt("dve", HINA[:, r + 1, :], TMP8[:, :], GA1[:, r, 0:8], ALU.add)
    P.memset("dve", HINRG[:, :], 0.0)
    for r in range(NCORE):
        P.stt("dve", HINRG[:, :], HINA[:, r, :], OHS[:, r:r + 1], HINRG[:, :], ALU.mult, ALU.add)
    stage(4)
    GG = UP
    for i in range(8):
        A_i = AB[:, i, :]
        B_i = AB[:, 8 + i, :]
        P.scan(HS[:, :], A_i, B_i, HINRG[:, i:i + 1])
        if i + 1 < 8:
            pre(("g", i + 1), 1024 + 128 * (i + 1))
        w = wq[("g", i)]
        for k in range(16):
            for ci, (c0, cn) in enumerate(ch2):
                P.mm(PS[ci][:, 0:cn], w[:, k, :], XN[:, k, 3 + c0:3 + c0 + cn], k == 0, k == 15)
        for ci, (c0, cn) in enumerate(ch2):
            P.act(GG[:, c0:c0 + cn], PS[ci][:, 0:cn], AF.Gelu_apprx_tanh)
        P.tt("dve", U[:, :], GG[:, 0:T], HS[:, :], ALU.mult)
        P.tt("pool", UBF[:, :], U[:, :], U[:, :], ALU.mult)
        for ci, (c0, cn) in enumerate(ch2):
            P.mm(PS[6 + ci][:, 0:cn], ONES[:, :], UBF[:, c0:c0 + cn], i == 0, i == 7, ms=True)
        P.act(MIXRG[:, i, :], U[:, :], AF.Copy, scale=V("grg", i))
        if i == 0:
            P.dump("yrg0", U[:, :])
    stats_rstd(PS[6:8], ch2, RSTD_RG[:, :], DRG)
    P.release(m_mixer)

    stage(5)
    RS = P.sb("rs", [128, 2, 32, 128], F32)
    off_rs = P.tinfo[RS.name][1]
    TAB = P.sb("tab", [128, 2, 32, 128], F32)
    off_tab = P.tinfo[TAB.name][1]
    KLT = P.sb("klt", [128, 8, 8, 128], BF16)
    S5S = P.sb("s5s", [128, 192], F32)
    S5C = P.sb("s5c", [128, 2, 1024], F32)
    APW = P.sb("apw", [128, 2, 9, 64], F32)
    PAIRC = P.sb("pairc", [128, 8, 32], F32)
    HSL = P.sb("hsl", [128, 64], F32)
    GA2 = P.sb("ga2", [128, NCORE, 64], F32)
    HIN2 = P.sb("hin2", [128, NCORE + 1, 64], F32)
    HINS = P.sb("hins", [128, 64], F32)
    m_s5 = P.mark()
    P.dma("sp", S5S[:, :], s5s_d.ap())
    P.dma("sp", S5C[:, :, :], s5c_d.ap())
    S5B = P.sb("s5b", [128, 2, 1024], F32)
    BB = P.sb("bb", [128, 2, 1024], F32)
    TS = [P.sb(f"ts{i}", [128, 64], F32) for i in range(8)]
    TI = P.sb("ti", [128, 64], I32)
    P.dma("sp", S5B[:, :, :], s5b_d.ap())
    ARE = S5S[:, 0:64]; AIM = S5S[:, 64:128]; LDT = S5S[:, 128:192]
    DT_, RE_, TH_ = TS[0], TS[1], TS[2]
    P.act(DT_[:, :], LDT, AF.Exp)
    P.tt("dve", RE_[:, :], ARE, DT_[:, :], ALU.mult)
    P.tt("dve", TH_[:, :], AIM, DT_[:, :], ALU.mult)

    def sincos(ang_ap, out_sin, out_cos, tmpa, tmpb, tmpi, n):
        for (dst, shift) in ((out_sin, 0.0), (out_cos, 0.5 * np.pi)):
            if shift != 0.0:
                P.ts("dve", tmpa, ang_ap, shift, None, ALU.add)
                src = tmpa
            else:
                src = ang_ap
            P.ts("dve", tmpi, src, 1.0 / TWO_PI, None, ALU.mult)
            P.copy("dve", tmpb, tmpi)
            P.stt("dve", tmpb, tmpb, -TWO_PI, src, ALU.mult, ALU.add)
            P.act(dst, tmpb, AF.Sin)

    ANG8 = P.sb("ang8", [128, 64], F32)
    for l in range(9):
        ang, rho, sn, cs = TS[3], TS[4], TS[5], TS[6]
        P.ts("dve", ang[:, :], TH_[:, :], float(l))
        P.act(rho[:, :], RE_[:, :], AF.Exp, scale=float(l))
        sincos(ang[:, :], sn[:, :], cs[:, :], TS[7][:, :], ANG8[:, :], TI[:, :], 64)
        P.tt("dve", APW[:, 0, l, :], rho[:, :], cs[:, :], ALU.mult)
        P.tt("dve", APW[:, 1, l, :], rho[:, :], sn[:, :], ALU.mult)
    P.ts("dve", TS[3][:, :], TH_[:, :], 8.0)
    P.ts("dve", TI[:, :], TS[3][:, :], 1.0 / TWO_PI, None, ALU.mult)
    P.copy("dve", ANG8[:, :], TI[:, :])
    P.stt("dve", ANG8[:, :], ANG8[:, :], -TWO_PI, TS[3][:, :], ALU.mult, ALU.add)
    XR, DEN, KR, KI = TS[3], TS[4], TS[5], TS[6]
    P.ts("dve", XR[:, :], APW[:, 0, 1, :], -1.0, None, ALU.add)
    P.tt("dve", DEN[:, :], ARE, ARE, ALU.mult)
    P.tt("dve", TS[7][:, :], AIM, AIM, ALU.mult)
    P.tt("dve", DEN[:, :], DEN[:, :], TS[7][:, :], ALU.add)
    P.op("dve", lambda e: e.reciprocal(out=DEN[:, :], in_=DEN[:, :]), [DEN[:, :]], [DEN[:, :]])
    P.tt("dve", KR[:, :], XR[:, :], ARE, ALU.mult)
    P.tt("dve", TS[7][:, :], APW[:, 1, 1, :], AIM, ALU.mult)
    P.tt("dve", KR[:, :], KR[:, :], TS[7][:, :], ALU.add)
    P.tt("dve", KR[:, :], KR[:, :], DEN[:, :], ALU.mult)
    P.tt("dve", KI[:, :], APW[:, 1, 1, :], ARE, ALU.mult)
    P.tt("dve", TS[7][:, :], XR[:, :], AIM, ALU.mult)
    P.tt("dve", KI[:, :], KI[:, :], TS[7][:, :], ALU.subtract)
    P.tt("dve", KI[:, :], KI[:, :], DEN[:, :], ALU.mult)
    b3 = lambda ap: ap.rearrange("p (g h) -> p g h", h=16)
    kb = lambda t: t[:, :].unsqueeze(2).to_broadcast([128, 64, 16])
    BT = P.sb("bt", [128, 1024], F32)
    P.tt("dve", b3(BB[:, 0, :]), b3(S5B[:, 0, :]), kb(KR), ALU.mult)
    P.tt("pool", b3(BT[:, :]), b3(S5B[:, 1, :]), kb(KI), ALU.mult)
    P.tt("dve", BB[:, 0, :], BB[:, 0, :], BT[:, :], ALU.subtract)
    P.tt("dve", b3(BB[:, 1, :]), b3(S5B[:, 1, :]), kb(KR), ALU.mult)
    P.tt("pool", b3(BT[:, :]), b3(S5B[:, 0, :]), kb(KI), ALU.mult)
    P.tt("dve", BB[:, 1, :], BB[:, 1, :], BT[:, :], ALU.add)
    def to_pair(dst, src):
        sv = src.rearrange("p (pp e) -> p pp e", e=2)
        P.ts("dve", dst, sv[:, :, 0], MTOP, None, ALU.mult)
        P.stt("dve", dst, sv[:, :, 1], MBOT, dst, ALU.mult, ALU.add)
    to_pair(PAIRC[:, 0, :], ANG8[:, :])
    RHO8 = TS[3]
    P.act(RHO8[:, :], RE_[:, :], AF.Exp, scale=8.0)
    to_pair(PAIRC[:, 1, :], RHO8[:, :])
    SQR, SQI, TQ = TS[4], TS[5], TS[6]
    P.copy("dve", SQR[:, :], APW[:, 0, 8, :])
    P.copy("dve", SQI[:, :], APW[:, 1, 8, :])
    for _ in range(7):
        P.tt("dve", TQ[:, :], SQR[:, :], SQI[:, :], ALU.mult)
        P.tt("dve", SQR[:, :], SQR[:, :], SQR[:, :], ALU.mult)
        P.tt("dve", SQI[:, :], SQI[:, :], SQI[:, :], ALU.mult)
        P.tt("dve", SQR[:, :], SQR[:, :], SQI[:, :], ALU.subtract)
        P.ts("dve", SQI[:, :], TQ[:, :], 2.0)
    to_pair(PAIRC[:, 2, :], SQR[:, :])
    to_pair(PAIRC[:, 3, :], SQI[:, :])
    ANG = RS[:, 0, :, :]
    ANB = RS[:, 1, :, :]
    _m_tib = P.mark()
    TIB = P.sb("tib", [128, 32, 128], I32)
    P.release(_m_tib)
    phib = PAIRC[:, 0, :].unsqueeze(2).to_broadcast([128, 32, 128])
    iob = IOTA.unsqueeze(1).to_broadcast([128, 32, 128])
    P.tt("dve", ANG, phib, iob, ALU.mult)
    for (dst, shift) in ((TAB[:, 1, :, :], 0.0), (TAB[:, 0, :, :], 0.5 * np.pi)):
        if shift != 0.0:
            P.ts("pool", ANG, ANG, shift, None, ALU.add)
        P.ts("dve", TIB[:, :, :], ANG, 1.0 / TWO_PI, None, ALU.mult)
        P.copy("dve", ANB, TIB[:, :, :])
        P.stt("dve", ANB, ANB, -TWO_PI, ANG, ALU.mult, ALU.add)
        P.act(dst, ANB, AF.Sin)
    CR = TAB[:, 0, :, :]; SR = TAB[:, 1, :, :]
    P.dump("cr", CR[:, 0, :]); P.dump("sr", SR[:, 0, :])
    P.dump("apw", APW[:, 0, :, :].rearrange("p l g -> p (l g)"))
    stage(6)
    PRI = P.sb("pri", [128, 2, 8, 128], F32)
    PT1 = P.sb("pt1", [128, 8, 128], F32)
    PSTK = P.sb("pstk", [128, 8, 128], F32)
    CSTK = P.sb("cstk", [128, 128], F32)
    CT1 = P.sb("ct1", [128, 128], F32)
    W1 = P.sb("w1", [128, 2, 8, 128], BF16)
    W1M = P.sb("w1m", [128, 2, 8, 128], BF16)
    RT = [P.sb(f"rt{i}", [128, 512], F32) for i in range(4)]
    GTMP = P.sb("gtmp", [128, 2, 128], F32)
    GL = P.sb("gl", [128, 64], F32)
    RHOB = lambda pp: PAIRC[:, 1, pp:pp + 1].to_broadcast([128, 128])
    for j in range(8):
        g0 = 8 * j
        apr = APW[:, 0, 0:8, g0:g0 + 8].unsqueeze(3).to_broadcast([128, 8, 8, 16])
        api = APW[:, 1, 0:8, g0:g0 + 8].unsqueeze(3).to_broadcast([128, 8, 8, 16])
        bbr = BB[:, 0, 16 * g0:16 * g0 + 128].rearrange("p (q h) -> p q h", h=16).unsqueeze(1).to_broadcast([128, 8, 8, 16])
        bbi = BB[:, 1, 16 * g0:16 * g0 + 128].rearrange("p (q h) -> p q h", h=16).unsqueeze(1).to_broadcast([128, 8, 8, 16])
        v4 = lambda ap: ap.rearrange("p l (q h) -> p l q h", h=16)
        P.tt("dve", v4(PRI[:, 0, :, :]), apr, bbr, ALU.mult)
        P.tt("pool", v4(PT1[:, :, :]), api, bbi, ALU.mult)
        P.tt("dve", PRI[:, 0, :, :], PRI[:, 0, :, :], PT1[:, :, :], ALU.subtract)
        P.tt("dve", v4(PRI[:, 1, :, :]), apr, bbi, ALU.mult)
        P.tt("pool", v4(PT1[:, :, :]), api, bbr, ALU.mult)
        P.tt("dve", PRI[:, 1, :, :], PRI[:, 1, :, :], PT1[:, :, :], ALU.add)
        P.ts("pool", PT1[:, :, :], PRI[:, 0, :, :], MTOP, None, ALU.mult)
        P.stt("dve", PSTK[:, :, :], PRI[:, 1, :, :], MBOT, PT1[:, :, :], ALU.mult, ALU.add)
        P.ts("pool", CT1[:, :], S5C[:, 0, 128 * j:128 * j + 128], MTOP, None, ALU.mult)
        P.ts("dve", CSTK[:, :], S5C[:, 1, 128 * j:128 * j + 128], MBOT, -1.0, ALU.mult, ALU.mult)
        P.tt("dve", CSTK[:, :], CSTK[:, :], CT1[:, :], ALU.add)
        for l in range(8):
            P.mm(PS[l // 4][:, (l % 4) * 128:(l % 4) * 128 + 128], PSTK[:, l, :], CSTK[:, :], True, True)
        P.stt("dve", CT1[:, :], IDENT, V("s5d", j), PS[0][:, 0:128], ALU.mult, ALU.add)
        P.tt("dve", KLT[:, j, 0, :], CT1[:, :], BDM, ALU.mult)
        bdb = BDM.unsqueeze(1).to_broadcast([128, 3, 128])
        P.tt("dve", KLT[:, j, 1:4, :], PS[0][:, 128:512].rearrange("p (l m) -> p l m", m=128), bdb, ALU.mult)
        bdb4 = BDM.unsqueeze(1).to_broadcast([128, 4, 128])
        P.tt("dve", KLT[:, j, 4:8, :], PS[1][:, :].rearrange("p (l m) -> p l m", m=128), bdb4, ALU.mult)
        for x in range(2):
            for l in range(8):
                P.transpose(PS[2 + x][:, l * 64:l * 64 + 64], PRI[0:64, x, l, :], IDENT[0:64, 0:64])
            tv = PS[2 + x][:, :].rearrange("p (l n) -> p l n", n=64)
            P.ts("dve", W1[:, x, :, 0:64], tv, MEVEN, None, ALU.mult)
            P.act(W1[:, x, :, 64:128], tv, AF.Copy, scale=MODD)
            P.ts("pool", W1M[:, x, :, :], W1[:, x, :, :], M96, None, ALU.mult)
        for pp in range(4):
            for x in range(2):
                for s in range(8):
                    if pp < 3:
                        P.mm(PS[4 + x][:, pp * 128:pp * 128 + 128], W1[32 * pp:32 * pp + 32, x, 7 - s, :],
                             UB[32 * pp:32 * pp + 32, j, s::8], s == 0, s == 7)
                    else:
                        P.mm(PS[4 + x][:, pp * 128:pp * 128 + 128], W1M[64:128, x, 7 - s, :],
                             UB[64:128, j, s::8], s == 0, s == 7)
        crj = CR[:, 4 * j:4 * j + 4, :]; srj = SR[:, 4 * j:4 * j + 4, :]
        v3 = lambda ap: ap.rearrange("p (a c) -> p a c", c=128)
        P.tt("dve", v3(RT[0][:, :]), v3(PS[4][:, :]), crj, ALU.mult)
        P.tt("dve", v3(RT[1][:, :]), v3(PS[5][:, :]), srj, ALU.mult)
        P.tt("pool", RS[:, 0, 4 * j:4 * j + 4, :], v3(RT[0][:, :]), v3(RT[1][:, :]), ALU.add)
        P.tt("dve", v3(RT[2][:, :]), v3(PS[5][:, :]), crj, ALU.mult)
        P.tt("dve", v3(RT[3][:, :]), v3(PS[4][:, :]), srj, ALU.mult)
        P.tt("pool", RS[:, 1, 4 * j:4 * j + 4, :], v3(RT[2][:, :]), v3(RT[3][:, :]), ALU.subtract)
        for pp in range(4):
            PP = 4 * j + pp
            for x in range(2):
                P.scan(GTMP[:, x, :], RHOB(PP), RS[:, x, PP, :], 0.0)
                P.copy("act", GL[:, 32 * x + PP:32 * x + PP + 1], GTMP[:, x, 127:128])
        if j == 0:
            P.dump("klt0", KLT[:, 0, :, :].rearrange("p l m -> p (l m)"), BF16)
            P.dump("w1", W1[:, :, :, :].rearrange("p x l m -> p (x l m)"), BF16)
            P.dump("rs0", RS[:, 0, 0, :])
    stage(7)
    c127 = CR[:, :, 127]; s127 = SR[:, :, 127]
    P.tt("dve", TS[0][:, 0:32], c127, GL[:, 0:32], ALU.mult)
    P.tt("dve", TS[1][:, 0:32], s127, GL[:, 32:64], ALU.mult)
    P.tt("dve", HSL[:, 0:32], TS[0][:, 0:32], TS[1][:, 0:32], ALU.subtract)
    P.tt("dve", TS[0][:, 0:32], s127, GL[:, 0:32], ALU.mult)
    P.tt("dve", TS[1][:, 0:32], c127, GL[:, 32:64], ALU.mult)
    P.tt("dve", HSL[:, 32:64], TS[0][:, 0:32], TS[1][:, 0:32], ALU.add)
    P.dma("sp", ib2.ap(), HSL[:, :])
    P.collective(ib2, ob2)
    P.dma("sp", GA2[:, :, :], ob2.ap().rearrange("(r p) f -> p r f", p=128))
    P.memset("dve", HIN2[:, 0, :], 0.0)
    A1r = PAIRC[:, 2, :]; A1i = PAIRC[:, 3, :]
    for r in range(NCORE):
        hr = HIN2[:, r, 0:32]; hi = HIN2[:, r, 32:64]
        P.tt("dve", TS[0][:, 0:32], A1r, hr, ALU.mult)
        P.tt("dve", TS[1][:, 0:32], A1i, hi, ALU.mult)
        P.tt("dve", TS[0][:, 0:32], TS[0][:, 0:32], TS[1][:, 0:32], ALU.subtract)
        P.tt("dve", HIN2[:, r + 1, 0:32], TS[0][:, 0:32], GA2[:, r, 0:32], ALU.add)
        P.tt("dve", TS[0][:, 0:32], A1i, hr, ALU.mult)
        P.tt("dve", TS[1][:, 0:32], A1r, hi, ALU.mult)
        P.tt("dve", TS[0][:, 0:32], TS[0][:, 0:32], TS[1][:, 0:32], ALU.add)
        P.tt("dve", HIN2[:, r + 1, 32:64], TS[0][:, 0:32], GA2[:, r, 32:64], ALU.add)
    P.memset("dve", HINS[:, :], 0.0)
    for r in range(NCORE):
        P.stt("dve", HINS[:, :], HIN2[:, r, :], OHS[:, r:r + 1], HINS[:, :], ALU.mult, ALU.add)
    P.release(m_s5)
    stage(8)
    HB = P.sb("hb", [128, 2, 32, 128], BF16)
    CA = P.sb("ca", [128, 2, 8, 128], F32)
    CT2 = P.sb("ct2", [128, 8, 128], F32)
    W3 = P.sb("w3", [128, 2, 4, 8, 32], BF16)
    W3Z = P.sb("w3z", [128, 2, 8, 64], BF16)
    P.memset("dve", W3Z[:, :, :, :], 0.0)
    GF = P.sb("gf", [128, 2, 4, 128], F32)
    RT2 = [P.sb(f"rtb{i}", [128, 512], F32) for i in range(4)]
    WG = [P.sb(f"wg{i}", [128, 8, 128], BF16) for i in range(2)]
    SG = P.sb("sg", [128, T], F32)
    YS = P.sb("ys", [128, T], F32)
    YQ = P.sb("yq", [128, T], BF16)
    for j in range(8):
        for pp in range(4):
            PP = 4 * j + pp
            for x in range(2):
                P.scan(GF[:, x, pp, :], RHOB(PP), RS[:, x, PP, :], HINS[:, 32 * x + PP:32 * x + PP + 1])
        crj = CR[:, 4 * j:4 * j + 4, 0:127]; srj = SR[:, 4 * j:4 * j + 4, 0:127]
        v3 = lambda ap: ap.rearrange("p (a c) -> p a c", c=128)[:, :, 0:127]
        gr = GF[:, 0, :, 0:127]; gi = GF[:, 1, :, 0:127]
        P.tt("dve", v3(RT2[0][:, :]), gr, crj, ALU.mult)
        P.tt("pool", v3(RT2[1][:, :]), gi, srj, ALU.mult)
        P.tt("dve", HB[:, 0, 4 * j:4 * j + 4, 1:128], v3(RT2[0][:, :]), v3(RT2[1][:, :]), ALU.subtract)
        P.tt("dve", v3(RT2[2][:, :]), gr, srj, ALU.mult)
        P.tt("pool", v3(RT2[3][:, :]), gi, crj, ALU.mult)
        P.tt("dve", HB[:, 1, 4 * j:4 * j + 4, 1:128], v3(RT2[2][:, :]), v3(RT2[3][:, :]), ALU.add)
    P.copy("dve", HB[:, 0, :, 0], HINS[:, 0:32])
    P.copy("dve", HB[:, 1, :, 0], HINS[:, 32:64])
    P.dump("hb0", HB[:, 0, 0, :], BF16)
    stage(9)
    Z = P.sb("z", [128, 8, T], F32, off=off_rs)
    ZB = P.sb("zb", [128, 8, T], BF16, off=off_tab)
    for j in range(8):
        g0 = 8 * j
        cr4 = S5C[:, 0, 128 * j:128 * j + 128].rearrange("p (q h) -> p q h", h=16).unsqueeze(1).to_broadcast([128, 8, 8, 16])
        ci4 = S5C[:, 1, 128 * j:128 * j + 128].rearrange("p (q h) -> p q h", h=16).unsqueeze(1).to_broadcast([128, 8, 8, 16])
        apr = APW[:, 0, 1:9, g0:g0 + 8].unsqueeze(3).to_broadcast([128, 8, 8, 16])
        api = APW[:, 1, 1:9, g0:g0 + 8].unsqueeze(3).to_broadcast([128, 8, 8, 16])
        v4 = lambda ap: ap.rearrange("p l (q h) -> p l q h", h=16)
        P.tt("dve", v4(CA[:, 0, :, :]), apr, cr4, ALU.mult)
        P.tt("pool", v4(CT2[:, :, :]), api, ci4, ALU.mult)
        P.tt("dve", CA[:, 0, :, :], CA[:, 0, :, :], CT2[:, :, :], ALU.subtract)
        P.tt("dve", v4(CA[:, 1, :, :]), api, cr4, ALU.mult)
        P.tt("pool", v4(CT2[:, :, :]), apr, ci4, ALU.mult)
        P.tt("dve", CA[:, 1, :, :], CA[:, 1, :, :], CT2[:, :, :], ALU.add)
        for x, sgn in ((0, 1.0), (1, -1.0)):
            cav = CA[:, x, :, :].rearrange("p l (pp e h) -> p l pp e h", e=2, h=16)
            for e_, msk in ((0, MTOP), (1, MBOT)):
                dst = W3[:, x, :, :, 16 * e_:16 * e_ + 16].rearrange("p pp l h -> p l pp h")
                P.ts("dve" if e_ == 0 else "pool", dst, cav[:, :, :, e_, :], msk, sgn, ALU.mult, ALU.mult)
        for x in range(2):
            P.copy("dve", W3Z[:, x, :, 32:64], W3[:, x, 3, :, :])
        for jp in range(8):
            bank = PS[jp // 4]
            cols = slice((jp % 4) * 128, (jp % 4) * 128 + 128)
            for s in range(jp + 1):
                P.mm(bank[:, cols], KLT[:, j, jp - s, :], UB[:, j, s::8], s == 0, False)
            for pp in range(4):
                PP = 4 * j + pp
                if pp < 3:
                    P.mm(bank[32 * pp:32 * pp + 32, cols], W3[:, 0, pp, jp, :], HB[:, 0, PP, :], False, False)
                    P.mm(bank[32 * pp:32 * pp + 32, cols], W3[:, 1, pp, jp, :], HB[:, 1, PP, :], False, True)
                else:
                    P.mm(bank[64:128, cols], W3Z[:, 0, jp, :], HB[:, 0, PP, :], False, False)
                    P.mm(bank[64:128, cols], W3Z[:, 1, jp, :], HB[:, 1, PP, :], False, True)
        zv = Z[:, j, :].rearrange("p (c jj) -> p jj c", jj=8)
        for hb_ in range(2):
            P.act(zv[:, 4 * hb_:4 * hb_ + 4, :], PS[hb_][:, :].rearrange("p (jj c) -> p jj c", c=128), AF.Gelu_apprx_tanh)
        P.copy("pool", ZB[:, j, :], Z[:, j, :])
        if j == 0:
            P.dump("w3", W3[:, :, :, :, :].rearrange("p x a l m -> p (x a l m)"), BF16)
            P.dump("z0", Z[:, 0, :])
    stage(10)
    MIXS5 = UB
    load_w(WG[0][:, :, :], w_glu, 0, 8, 0)
    for m in range(8):
        w = WG[m % 2]
        if m + 1 < 8:
            load_w(WG[(m + 1) % 2][:, :, :], w_glu, 0, 8, 128 * (m + 1))
        for k in range(8):
            for ci, (c0, cn) in enumerate(ch2):
                P.mm(PS[2 + ci][:, 0:cn], w[:, k, :], ZB[:, k, c0:c0 + cn], k == 0, k == 7)
        for ci, (c0, cn) in enumerate(ch2):
            P.act(SG[:, c0:c0 + cn], PS[2 + ci][:, 0:cn], AF.Sigmoid, bias=V("bglu", m))
        P.tt("dve", YS[:, :], Z[:, m, :], SG[:, :], ALU.mult)
        P.tt("pool", YQ[:, :], YS[:, :], YS[:, :], ALU.mult)
        for ci, (c0, cn) in enumerate(ch2):
            P.mm(PS[4 + ci][:, 0:cn], ONES[:, :], YQ[:, c0:c0 + cn], m == 0, m == 7, ms=True)
        P.act(MIXS5[:, m, :], YS[:, :], AF.Copy, scale=V("gs5", m))
        if m == 0:
            P.dump("ys0", YS[:, :])
    stats_rstd(PS[4:6], ch2, RSTD_S5[:, :], DS5)
    P.release(m_mixer)

    stage(11)
    X2 = P.sb("x2", [128, 16, T], F32)
    XN2 = P.sb("xn2", [128, 16, T + 2], BF16)
    m_ffn = P.mark()
    WO = [P.sb(f"wo{i}", [128, 16, 128], BF16) for i in range(2)]
    TO = [P.sb(f"to{i}", [128, T], F32) for i in range(2)]
    xv2 = xT.ap().rearrange("(k p) t -> p k t", p=128)
    for q in range(4):
        P.dma("sp", X2[:, 4 * q:4 * q + 4, :], xv2[:, 4 * q:4 * q + 4, 3:3 + T])
    load_w(WO[0][:, :, :], w_out, 0, 16, 0)
    for m in range(16):
        w = WO[m % 2]
        if m + 1 < 16:
            load_w(WO[(m + 1) % 2][:, :, :], w_out, 0, 16, 128 * (m + 1))
        for k in range(8):
            for ci, (c0, cn) in enumerate(ch2):
                P.mm(PS[ci][:, 0:cn], w[:, k, :], MIXRG[:, k, c0:c0 + cn], k == 0, k == 7)
        for k in range(8):
            for ci, (c0, cn) in enumerate(ch2):
                P.mm(PS[2 + ci][:, 0:cn], w[:, 8 + k, :], MIXS5[:, k, c0:c0 + cn], k == 0, k == 7)
        for ci, (c0, cn) in enumerate(ch2):
            P.tt("dve", TO[0][:, c0:c0 + cn], PS[ci][:, 0:cn], RSTD_RG[:, c0:c0 + cn], ALU.mult)
            P.tt("dve", TO[1][:, c0:c0 + cn], PS[2 + ci][:, 0:cn], RSTD_S5[:, c0:c0 + cn], ALU.mult)
        P.tt("pool", TO[0][:, :], TO[0][:, :], TO[1][:, :], ALU.add)
        P.tt("pool", X2[:, m, :], X2[:, m, :], TO[0][:, :], ALU.add)
    P.dump("xmid0", X2[:, 0, :])
    stage(12)
    HX = P.sb("hx", [128, 16, 2], F32)
    GA3 = P.sb("ga3", [128, NCORE, 32], F32)
    XH = P.sb("xh", [128, 16, 2], F32)
    P.copy("dve", HX[:, :, :], X2[:, :, T - 2:T])
    P.dma("sp", ib3.ap(), HX[:, :, :].rearrange("p k t -> p (k t)"))
    P.collective(ib3, ob3)
    P.dma("sp", GA3[:, :, :], ob3.ap().rearrange("(r p) f -> p r f", p=128))
    xhf = XH[:, :, :].rearrange("p k t -> p (k t)")
    P.memset("dve", xhf, 0.0)
    for r in range(NCORE):
        P.stt("dve", xhf, GA3[:, r, :], OHP[:, r:r + 1], xhf, ALU.mult, ALU.add)
    SQ2 = [P.sb(f"sqb{i}", [128, T], BF16) for i in range(2)]
    SQH = P.sb("sqh", [128, 16, 2], BF16)
    RB2 = P.sb("rb2", [128, T + 2], F32)
    P.tt("dve", SQH[:, :, :], XH[:, :, :], XH[:, :, :], ALU.mult)
    for k in range(16):
        sq = SQ2[k % 2]
        if k % 2 == 0:
            P.act(sq[:, :], X2[:, k, :], AF.Square)
        else:
            P.tt("pool", sq[:, :], X2[:, k, :], X2[:, k, :], ALU.mult)
        for ci, (c0, cn) in enumerate(ch2):
            P.mm(PS[ci][:, 0:cn], ONES[:, :], sq[:, c0:c0 + cn], k == 0, k == 15, ms=True)
        P.mm(PS[2][:, 0:2], ONES[:, :], SQH[:, k, :], k == 0, k == 15, ms=True)
    stats_rstd([PS[2], PS[0], PS[1]], [(0, 2), (2, 512), (514, 512)], RB2[:, :], D)
    for k in range(16):
        P.stt("dve", XN2[:, k, 0:2], XH[:, k, :], V("n2g", k), RB2[:, 0:2], ALU.mult, ALU.mult)
        P.stt("dve" if k % 2 == 0 else "pool", XN2[:, k, 2:], X2[:, k, :], V("n2g", k), RB2[:, 2:], ALU.mult, ALU.mult)
    P.dump("xn2", XN2[:, 0, :], BF16)
    P.release(m_ffn)

    stage(13)
    NQ = 12
    off_mix = P.tinfo[MIXRG.name][1]
    ACTT = P.sb("actt", [128, NQ, T], BF16, off=off_mix)
    WU = [P.sb(f"wu{i}", [128, 2, 16, 128], BF16) for i in range(2)]
    WD = [P.sb(f"wd{i}", [128, NQ, 128], BF16, off=off_mix + NQ * T * 2 + i * NQ * 256) for i in range(2)]
    GP = [P.sb(f"gp{i}", [128, T + 2], F32) for i in range(2)]
    VP = [P.sb(f"vp{i}", [128, T + 2], F32) for i in range(2)]
    Gc = [P.sb(f"gc{i}", [128, T], F32) for i in range(2)]
    Vc = [P.sb(f"vc{i}", [128, T], F32) for i in range(2)]
    chf = chunks_of(T + 2)
    wdi = 0

    def ld_wu(f_):
        w_ = WU[f_ % 2]
        load_w(w_[:, 0, :, :], w_up, 0, 16, 128 * f_)
        load_w(w_[:, 1, :, :], w_up, 0, 16, DFF + 128 * f_)

    def ld_wd(idx):
        qq_, m_ = idx // 16, idx % 16
        w_ = WD[idx % 2]
        src_ = w_down.ap()[idx]
        P.dma("pool", w_[:, :, :], src_, max_dma_last_dim=4096)
    NFT = DFF // 128
    ld_wu(0)
    for qq in range(NFT // NQ):
        for fl in range(NQ):
            f = qq * NQ + fl
            w = WU[f % 2]
            if f + 1 < NFT:
                ld_wu(f + 1)
            if fl == NQ - 1:
                ld_wd(16 * qq)
            gp, vp, gc, vc = GP[f % 2], VP[f % 2], Gc[f % 2], Vc[f % 2]
            for half in range(2):
                for k in range(16):
                    for ci, (c0, cn) in enumerate(chf):
                        P.mm(PS[3 * half + ci][:, 0:cn], w[:, half, k, :], XN2[:, k, c0:c0 + cn], k == 0, k == 15)
            for ci, (c0, cn) in enumerate(chf):
                P.copy("act", gp[:, c0:c0 + cn], PS[ci][:, 0:cn])
                P.copy("dve", vp[:, c0:c0 + cn], PS[3 + ci][:, 0:cn])
            for (src, dst, tile_, e0, e1) in ((gp, gc, f, "pool", "dve"), (vp, vc, 48 + f, "dve", "pool")):
                P.ts(e0, dst[:, :], src[:, 2:2 + T], V("fcw", 2 * 96 + tile_), V("fcb", tile_), ALU.mult, ALU.add)
                P.stt(e1, dst[:, :], src[:, 1:1 + T], V("fcw", 96 + tile_), dst[:, :], ALU.mult, ALU.add)
                P.stt(e0, dst[:, :], src[:, 0:T], V("fcw", tile_), dst[:, :], ALU.mult, ALU.add)
            P.act(gc[:, :], gc[:, :], AF.Gelu_apprx_tanh)
            P.tt("pool", ACTT[:, fl, :], gc[:, :], vc[:, :], ALU.mult)
            if f == 0:
                P.dump("act0", ACTT[:, 0, :], BF16)
        for m in range(16):
            w = WD[wdi % 2]
            wdi += 1
            if m + 1 < 16:
                ld_wd(16 * qq + m + 1)
            for fl in range(NQ):
                for ci, (c0, cn) in enumerate(ch2):
                    P.mm(PS[6 + ci][:, 0:cn], w[:, fl, :], ACTT[:, fl, c0:c0 + cn], fl == 0, fl == NQ - 1)
            for ci, (c0, cn) in enumerate(ch2):
                P.tt("dve", X2[:, m, c0:c0 + cn], X2[:, m, c0:c0 + cn], PS[6 + ci][:, 0:cn], ALU.add)
    P.release(m_ffn)
    stage(14)
    SQ3 = [P.sb(f"sqc{i}", [128, T], BF16) for i in range(2)]
    RB3 = P.sb("rb3", [128, T], F32)
    for k in range(16):
        sq = SQ3[k % 2]
        if k % 2 == 0:
            P.act(sq[:, :], X2[:, k, :], AF.Square)
        else:
            P.tt("pool", sq[:, :], X2[:, k, :], X2[:, k, :], ALU.mult)
        for ci, (c0, cn) in enumerate(ch2):
            P.mm(PS[ci][:, 0:cn], ONES[:, :], sq[:, c0:c0 + cn], k == 0, k == 15, ms=True)
    stats_rstd(PS[0:2], ch2, RB3[:, :], D)
    yv = yT.ap().rearrange("(k p) t -> p k t", p=128)
    for k in range(16):
        P.stt("dve" if k % 2 == 0 else "pool", X2[:, k, :], X2[:, k, :], V("fng", k), RB3[:, :], ALU.mult, ALU.mult)
        if k % 4 == 3:
            P.dma("sp", yv[:, k - 3:k + 1, :], X2[:, k - 3:k + 1, :])
    return


def prep_inputs(inp):
    f = np.float32
    x = np.asarray(inp["x"], f)[0]
    xT_full = np.ascontiguousarray(x.T)
    xpad = np.concatenate([np.zeros((D, 3), f), xT_full], axis=1)

    def colT(v):
        v = np.asarray(v, f).reshape(-1)
        return v.reshape(-1, 128).T
    items = {
        "n1g": inp["norm1_g"][0], "rgcw": inp["rg_conv_w"][0], "rgcb": inp["rg_conv_b"][0],
        "ba": inp["rg_ba"][0], "bi": inp["rg_bi"][0], "lam": inp["rg_lambda"][0],
        "s5d": inp["s5_d"][0], "bglu": inp["s5_b_glu"][0], "grg": inp["out_norm_rg_g"][0],
        "gs5": inp["out_norm_s5_g"][0], "n2g": inp["norm2_g"][0], "fcw": inp["ffn_conv_w"][0],
        "fcb": inp["ffn_conv_b"][0], "fng": inp["final_norm_g"],
    }
    vecT = np.ascontiguousarray(np.concatenate([colT(items[n]) for n, _ in VEC_ITEMS], axis=1)).astype(f)
    assert vecT.shape == (128, NV)
    p = np.arange(128)
    cst0 = np.zeros((128, NCST), f)
    cst0[:, C_ID:C_ID + 128] = np.eye(128, dtype=f)
    cst0[:, C_BD:C_BD + 128] = (p[:, None] // 16 == p[None, :] // 16).astype(f)
    cst0[:, C_IOTA:C_IOTA + 128] = np.arange(1, 129, dtype=f)[None, :]
    cst0[:, C_M] = (p < 64); cst0[:, C_M + 1] = (p >= 64)
    cst0[:, C_M + 2] = ((p // 16) % 2 == 0); cst0[:, C_M + 3] = ((p // 16) % 2 == 1)
    cst0[:, C_M96] = (p >= 96)
    dup = lambda a: np.concatenate([a, a], axis=0)
    s5s = dup(np.concatenate([np.asarray(inp["s5_a_re"][0], f).T, np.asarray(inp["s5_a_im"][0], f).T,
                              np.broadcast_to(np.asarray(inp["s5_log_dt"][0], f)[None, :], (64, 64))], axis=1))
    bre = np.asarray(inp["s5_b_re"][0], f).transpose(1, 0, 2).reshape(64, 1024)
    bim = np.asarray(inp["s5_b_im"][0], f).transpose(1, 0, 2).reshape(64, 1024)
    cre = np.asarray(inp["s5_c_re"][0], f).transpose(2, 0, 1).reshape(64, 1024)
    cim = np.asarray(inp["s5_c_im"][0], f).transpose(2, 0, 1).reshape(64, 1024)
    s5b = dup(np.stack([bre, bim], axis=1))
    s5c = dup(np.stack([cre, cim], axis=1))

    def bd(wh):
        wh = np.asarray(wh, f)
        o = np.zeros((128, 8, 128), f)
        for i in range(8):
            o[0:64, i, 0:64] = wh[2 * i]
            o[64:128, i, 64:128] = wh[2 * i + 1]
        return o
    def tile_w(w):
        w = np.asarray(w, f)
        kt, nt = w.shape[0] // 128, w.shape[1] // 128
        return np.ascontiguousarray(w.reshape(kt, 128, nt, 128).transpose(2, 1, 0, 3))
    shared = {
        "vecT": vecT, "s5s": np.ascontiguousarray(s5s), "s5b": np.ascontiguousarray(s5b),
        "s5c": np.ascontiguousarray(s5c), "wabd": bd(inp["rg_wa"][0]), "wibd": bd(inp["rg_wi"][0]),
        "w_in": tile_w(inp["w_in"][0]),
        "w_glu": tile_w(inp["s5_w_glu"][0]),
        "w_out": tile_w(inp["w_out"][0]),
        "w_up": tile_w(inp["ffn_w_up"][0]),
        "w_down": np.ascontiguousarray(np.asarray(inp["ffn_w_down"][0], f).reshape(4, 12, 128, 16, 128)
                                       .transpose(0, 3, 2, 1, 4).reshape(64, 128, 12, 128)),
    }
    maps = []
    for c in range(NCORE):
        cst = cst0.copy()
        cst[:, C_OHS + c] = 1.0
        if c > 0:
            cst[:, C_OHP + c - 1] = 1.0
        m = dict(shared)
        m["cst"] = cst
        m["xT"] = np.ascontiguousarray(xpad[:, T * c:T * c + T + 3])
        maps.append(m)
    return maps


_CACHE = {}


def run(inputs, debug=(), stop=99):
    key = (tuple(sorted(debug)), stop)
    if key not in _CACHE:
        _CACHE[key] = build_program(debug, stop)
    nc, P = _CACHE[key]
    maps = prep_inputs(inputs)
    if stop < 13:
        for m in maps:
            m["w_up"] = np.zeros((2, 128, 16, 128), np.float32)
            m["w_down"] = np.zeros((2, 128, 12, 128), np.float32)
    res = run_bass_kernel_spmd(nc, maps, core_ids=list(range(NCORE)))
    return res, P


def kernel(**inputs):
    res, P = run(inputs)
    outT = np.concatenate([np.asarray(r["yT"], np.float32) for r in res.results], axis=1)
    return np.ascontiguousarray(outT.T)[None, :, :].astype(np.float32)
```
